# Optimizing a Trainium2 kernel written in Bass

```python
import math
import jax, jax.numpy as jnp
from jax import lax
import numpy as np

D_MODEL = 1024
BATCH = 32
SEQ = 2048
DEPTH = 4

SSM_HEAD_DIM = 64
SSM_HEADS = D_MODEL // SSM_HEAD_DIM
SSM_WIDTH = SSM_HEADS * SSM_HEAD_DIM
SSM_GROUPS = 2
SSM_STATE = 128
SSM_CONV = 7
SSM_CHUNK = 128
ATTN_HEAD_DIM = 64
ATTN_HEADS = D_MODEL // ATTN_HEAD_DIM
ATTN_KV_HEADS = ATTN_HEADS // 4
ATTN_WIDTH = ATTN_HEADS * ATTN_HEAD_DIM
KV_WIDTH = ATTN_KV_HEADS * ATTN_HEAD_DIM
WINDOW = 128
ATTN_BLOCK = 128
KEY_SPAN = ATTN_BLOCK + 2 * WINDOW
REL_BUCKETS = 32
REL_MAX_DIST = 128
MIX_WIDTH = SSM_WIDTH + ATTN_WIDTH
D_FF = 256 * ((8 * D_MODEL // 3 + 255) // 256)
FFN_CONV = 3
NORM_EPS = 1e-6

BC_WIDTH = SSM_GROUPS * SSM_STATE
CONV_CH = SSM_WIDTH + 2 * BC_WIDTH
Z_END = SSM_WIDTH
XBC_END = Z_END + CONV_CH
DT_END = XBC_END + 2 * SSM_HEADS
Q_END = DT_END + ATTN_WIDTH
K_END = Q_END + KV_WIDTH
IN_COLS = K_END + KV_WIDTH

kernel_name = "hymba_ssd_swa_convffn_encoder"


def rms_norm(x, w):
    xf = x.astype(jnp.float32)
    y = xf * lax.rsqrt(jnp.mean(xf * xf, axis=-1, keepdims=True) + NORM_EPS)
    return (y * w.astype(jnp.float32)).astype(x.dtype)


def depthwise_conv_centered(x, w, b):
    k, ch = w.shape
    pad = k // 2
    y = lax.conv_general_dilated(
        x, w[:, None, :].astype(x.dtype), window_strides=(1,), padding=[(pad, pad)],
        dimension_numbers=("NWC", "WIO", "NWC"), feature_group_count=ch)
    return y + b.astype(x.dtype)


def t5_bucket(rel):
    half = REL_BUCKETS // 2
    max_exact = half // 2
    ret = jnp.where(rel > 0, half, 0)
    n = jnp.abs(rel)
    nf = jnp.maximum(n, 1).astype(jnp.float32)
    large = max_exact + (jnp.log(nf / max_exact) / math.log(REL_MAX_DIST / max_exact)
                         * (half - max_exact)).astype(jnp.int32)
    large = jnp.minimum(large, half - 1)
    return ret + jnp.where(n < max_exact, n, large)


def ssd_chunked(x, dt, a, b, c):
    f32 = jnp.float32
    bsz, seq, nh, hp = x.shape
    ng, ns = b.shape[-2:]
    rep = nh // ng
    nc, cl = seq // SSM_CHUNK, SSM_CHUNK
    xdt = (x.astype(f32) * dt[..., None]).reshape(bsz, nc, cl, ng, rep, hp)
    a_dt = (dt * a).reshape(bsz, nc, cl, ng, rep).transpose(0, 3, 4, 1, 2)
    a_cum = jnp.cumsum(a_dt, axis=-1)
    b = b.astype(f32).reshape(bsz, nc, cl, ng, ns)
    c = c.astype(f32).reshape(bsz, nc, cl, ng, ns)
    seg = a_cum[..., :, None] - a_cum[..., None, :]
    lower = jnp.tril(jnp.ones((cl, cl), dtype=bool))
    decay = jnp.exp(jnp.where(lower, seg, -jnp.inf))
    cb = jnp.einsum("bclgn,bcsgn->bgcls", c, b)
    mix = cb[:, :, None] * decay
    y_diag = jnp.einsum("bgrcls,bcsgrp->bclgrp", mix, xdt)
    decay_states = jnp.exp(a_cum[..., -1:] - a_cum)
    states = jnp.einsum("bclgn,bgrcl,bclgrp->cbgrpn", b, decay_states, xdt)
    chunk_decay = jnp.exp(a_cum[..., -1]).transpose(3, 0, 1, 2)

    def step(h, inp):
        st, dec = inp
        return h * dec[..., None, None] + st, h

    h0 = jnp.zeros((bsz, ng, rep, hp, ns), f32)
    _, prev = lax.scan(step, h0, (states, chunk_decay))
    y_off = jnp.einsum("bclgn,cbgrpn,bgrcl->bclgrp", c, prev, jnp.exp(a_cum))
    return (y_diag + y_off).reshape(bsz, seq, nh, hp)


def windowed_gqa(q, k, v, sink, band_bias):
    bsz, seq, nh, hd = q.shape
    nkv = k.shape[2]
    rep = nh // nkv
    nblk = seq // ATTN_BLOCK
    qb_all = q.reshape(bsz, nblk, ATTN_BLOCK, nkv, rep, hd).transpose(1, 0, 2, 3, 4, 5)
    kp = jnp.pad(k, ((0, 0), (WINDOW, WINDOW), (0, 0), (0, 0)))
    vp = jnp.pad(v, ((0, 0), (WINDOW, WINDOW), (0, 0), (0, 0)))
    rel = jnp.arange(KEY_SPAN)[None, :] - WINDOW - jnp.arange(ATTN_BLOCK)[:, None]
    band = jnp.abs(rel) <= WINDOW
    bias = band_bias.reshape(nkv, rep, ATTN_BLOCK, KEY_SPAN)
    sink_l = sink.astype(jnp.float32).reshape(nkv, rep, 1, 1)
    scale = hd ** -0.5

    def one_block(args):
        qb, n = args
        start = n * ATTN_BLOCK
        kb = lax.dynamic_slice_in_dim(kp, start, KEY_SPAN, axis=1)
        vb = lax.dynamic_slice_in_dim(vp, start, KEY_SPAN, axis=1)
        kpos = start - WINDOW + jnp.arange(KEY_SPAN)
        valid = band & ((kpos >= 0) & (kpos < seq))[None, :]
        s = jnp.einsum("bqgrd,bkgd->bgrqk", qb, kb,
                       preferred_element_type=jnp.float32) * scale + bias
        s = jnp.where(valid, s, -jnp.inf)
        m = jnp.maximum(jnp.max(s, axis=-1, keepdims=True), sink_l)
        p = jnp.exp(s - m)
        denom = jnp.sum(p, axis=-1, keepdims=True) + jnp.exp(sink_l - m)
        return jnp.einsum("bgrqk,bkgd->bqgrd", (p / denom).astype(vb.dtype), vb)

    out = lax.map(one_block, (qb_all, jnp.arange(nblk)))
    return out.transpose(1, 0, 2, 3, 4, 5).reshape(bsz, seq, nh * hd)


def setup_inputs(seed: int = 0) -> dict:
    key = jax.random.key(seed)
    ks = jax.random.split(key, 20)
    f32 = jnp.float32

    def nrm(k, shape, scale):
        return jax.random.normal(k, shape, f32) * scale

    dt0 = jnp.exp(jax.random.uniform(ks[6], (DEPTH, 2, SSM_HEADS), f32,
                                     math.log(1e-3), math.log(1e-1)))
    return {
        "x": nrm(ks[0], (BATCH, SEQ, D_MODEL), 1.0),
        "rel_bias": nrm(ks[1], (REL_BUCKETS, ATTN_HEADS), 0.5),
        "norm1_w": 1.0 + nrm(ks[2], (DEPTH, D_MODEL), 0.05),
        "w_in": nrm(ks[3], (DEPTH, D_MODEL, IN_COLS), D_MODEL ** -0.5),
        "conv_w": nrm(ks[4], (DEPTH, SSM_CONV, CONV_CH), SSM_CONV ** -0.5),
        "conv_b": nrm(ks[5], (DEPTH, CONV_CH), 0.01),
        "dt_bias": dt0 + jnp.log(-jnp.expm1(-dt0)),
        "a_log": jnp.log(jax.random.uniform(ks[7], (DEPTH, 2, SSM_HEADS), f32, 1.0, 16.0)),
        "d_skip": 1.0 + nrm(ks[8], (DEPTH, SSM_HEADS), 0.1),
        "ssm_norm_w": 1.0 + nrm(ks[9], (DEPTH, SSM_WIDTH), 0.05),
        "attn_sink": nrm(ks[10], (DEPTH, ATTN_HEADS), 0.5),
        "w_out": nrm(ks[11], (DEPTH, MIX_WIDTH, D_MODEL), MIX_WIDTH ** -0.5),
        "norm2_w": 1.0 + nrm(ks[12], (DEPTH, D_MODEL), 0.05),
        "w_up": nrm(ks[13], (DEPTH, D_MODEL, 2 * D_FF), D_MODEL ** -0.5),
        "ffn_conv_w": nrm(ks[14], (DEPTH, FFN_CONV, D_FF), FFN_CONV ** -0.5),
        "ffn_conv_b": nrm(ks[15], (DEPTH, D_FF), 0.01),
        "w_down": nrm(ks[16], (DEPTH, D_FF, D_MODEL), D_FF ** -0.5),
        "final_norm_w": 1.0 + nrm(ks[17], (D_MODEL,), 0.05),
    }


def reference(x, rel_bias, norm1_w, w_in, conv_w, conv_b, dt_bias, a_log, d_skip,
              ssm_norm_w, attn_sink, w_out, norm2_w, w_up, ffn_conv_w, ffn_conv_b,
              w_down, final_norm_w):
    bsz, seq, _ = x.shape
    f32 = jnp.float32
    rel = jnp.arange(KEY_SPAN)[None, :] - WINDOW - jnp.arange(ATTN_BLOCK)[:, None]
    band_bias = rel_bias.astype(f32)[t5_bucket(rel)].transpose(2, 0, 1)

    for i in range(DEPTH):
        h = rms_norm(x, norm1_w[i])
        proj = h @ w_in[i]
        z, xbc, dt_raw, q, k, v = jnp.split(proj, [Z_END, XBC_END, DT_END, Q_END, K_END], axis=-1)

        xbc = jax.nn.silu(depthwise_conv_centered(xbc, conv_w[i], conv_b[i]))
        xs, bm, cm = jnp.split(xbc, [SSM_WIDTH, SSM_WIDTH + BC_WIDTH], axis=-1)
        xs = xs.reshape(bsz, seq, SSM_HEADS, SSM_HEAD_DIM)
        bm = bm.reshape(bsz, seq, SSM_GROUPS, SSM_STATE)
        cm = cm.reshape(bsz, seq, SSM_GROUPS, SSM_STATE)
        dt = jax.nn.softplus(dt_raw.astype(f32).reshape(bsz, seq, 2, SSM_HEADS)
                             + dt_bias[i].astype(f32))
        a = -jnp.exp(a_log[i].astype(f32))
        y_fwd = ssd_chunked(xs, dt[:, :, 0], a[0], bm, cm)
        y_bwd = jnp.flip(ssd_chunked(jnp.flip(xs, 1), jnp.flip(dt[:, :, 1], 1), a[1],
                                     jnp.flip(bm, 1), jnp.flip(cm, 1)), 1)
        y_ssm = y_fwd + y_bwd + d_skip[i].astype(f32)[:, None] * xs.astype(f32)
        y_ssm = y_ssm.reshape(bsz, seq, SSM_GROUPS, SSM_WIDTH // SSM_GROUPS) \
            * jax.nn.silu(z.astype(f32)).reshape(bsz, seq, SSM_GROUPS, SSM_WIDTH // SSM_GROUPS)
        y_ssm = rms_norm(y_ssm, ssm_norm_w[i].reshape(SSM_GROUPS, -1)).reshape(bsz, seq, SSM_WIDTH)

        y_attn = windowed_gqa(q.reshape(bsz, seq, ATTN_HEADS, ATTN_HEAD_DIM),
                              k.reshape(bsz, seq, ATTN_KV_HEADS, ATTN_HEAD_DIM),
                              v.reshape(bsz, seq, ATTN_KV_HEADS, ATTN_HEAD_DIM),
                              attn_sink[i], band_bias)

        mixed = jnp.concatenate([y_ssm.astype(x.dtype), y_attn.astype(x.dtype)], axis=-1)
        x = x + mixed @ w_out[i]

        h = rms_norm(x, norm2_w[i])
        g, u = jnp.split(h @ w_up[i], [D_FF], axis=-1)
        g = depthwise_conv_centered(g, ffn_conv_w[i], ffn_conv_b[i])
        x = x + (jax.nn.silu(g) * u) @ w_down[i]

    return rms_norm(x, final_norm_w)
```

```python
import contextlib
import numpy as np
import concourse.bass as bass
import concourse.mybir as mybir
from concourse.bass_utils import run_bass_kernel_spmd

F32 = mybir.dt.float32
BF16 = mybir.dt.bfloat16
AF = mybir.ActivationFunctionType
ALU = mybir.AluOpType
AX = mybir.AxisListType

NCORES = 8
L = 2048
D = 1024
DEPTH = 4
DFF = 2816
EPS = 1e-6

ENGS = ("pe", "act", "dve", "pool", "sp")
NDMA_SEMS = 12


class Buf:
    __slots__ = ("name", "writer", "readers")

    def __init__(self, name):
        self.name = name
        self.writer = None
        self.readers = []


class Op:
    __slots__ = ("eng", "fn", "deps", "dma", "signals", "sem", "val", "slotwait")

    def __init__(self, eng, fn, dma):
        self.eng = eng
        self.fn = fn
        self.dma = dma
        self.deps = []
        self.signals = False
        self.sem = None
        self.val = 0
        self.slotwait = None


class Prog:
    def __init__(self, nc):
        self.nc = nc
        self.ops = {e: [] for e in ENGS}
        self.all_dma = []
        self.nbuf = 0
        self.recent = []

    def buf(self, name=None):
        self.nbuf += 1
        return Buf(name or f"b{self.nbuf}")

    def bufs(self, n, name="b"):
        return [self.buf(f"{name}{i}") for i in range(n)]

    def add(self, eng, fn, reads=(), writes=(), dma=False):
        op = Op(eng, fn, dma)
        deps = []
        for r in reads:
            if r.writer is not None:
                deps.append((r.writer, "raw"))
        for w in writes:
            if w.writer is not None:
                deps.append((w.writer, "waw"))
            lastrd = {}
            for rd in w.readers:
                if rd.dma:
                    deps.append((rd, "war"))
                else:
                    lastrd[rd.eng] = rd
            for rd in lastrd.values():
                deps.append((rd, "war"))
        seen = set()
        for d, kind in deps:
            if d is op or id(d) in seen:
                continue
            if d.eng == eng and not d.dma and not dma:
                if eng == "pe" or kind != "raw":
                    continue
            seen.add(id(d))
            op.deps.append(d)
            d.signals = True
        for r in reads:
            r.readers.append(op)
        for w in writes:
            w.writer = op
            w.readers = []
        self.ops[eng].append(op)
        if dma:
            op.signals = True
            self.all_dma.append(op)
        self.recent.append(op)
        return op

    def dma(self, eng, out, in_, reads=(), writes=()):
        return self.add(eng, R.dma_start(out=out, in_=in_), reads, writes, dma=True)

    def barrier(self):
        lasts = []
        for e in ENGS:
            comp = [o for o in self.ops[e] if not o.dma]
            if comp:
                lasts.append(comp[-1])
        dmas = [o for o in self.recent if o.dma]
        self.recent = []
        bb = self.buf("barrier")
        for o in lasts + dmas:
            o.signals = True
        self._pending_barrier = lasts + dmas
        self._barrier_seen = {e: False for e in ENGS}

    def _apply_barrier(self, op):
        pb = getattr(self, "_pending_barrier", None)
        if pb and not self._barrier_seen[op.eng]:
            self._barrier_seen[op.eng] = True
            have = set(id(d) for d in op.deps)
            for d in pb:
                if id(d) not in have and d is not op:
                    if d.eng == op.eng and not d.dma:
                        continue
                    op.deps.append(d)

    def emit(self, final_wait_eng="sp"):
        nc = self.nc
        with contextlib.ExitStack() as es:
            SEM_MAX = 60000
            nsig = {e: sum(1 for o in self.ops[e] if o.signals and not o.dma) for e in ENGS}
            esem = {e: [es.enter_context(nc.semaphore(f"s_{e}{k}")) for k in range(nsig[e] // SEM_MAX + 1)] for e in ENGS}
            dsem = {}
            for e in ENGS:
                if any(o.dma for o in self.ops[e]):
                    dsem[e] = [es.enter_context(nc.semaphore(f"d_{e}{i}")) for i in range(NDMA_SEMS)]
            for e in ENGS:
                cnt = 0
                nd = 0
                for op in self.ops[e]:
                    if op.dma:
                        k = nd % NDMA_SEMS
                        op.sem = dsem[e][k]
                        op.val = 16 * (nd // NDMA_SEMS + 1)
                        if nd >= NDMA_SEMS:
                            op.slotwait = (op.sem, op.val - 16)
                        nd += 1
                    elif op.signals:
                        op.sem = esem[e][cnt // SEM_MAX]
                        op.val = cnt % SEM_MAX + 1
                        cnt += 1
            block = es.enter_context(nc.Block())
            engobj = {"pe": block.tensor, "act": block.scalar, "dve": block.vector,
                      "pool": block.gpsimd, "sp": block.sync}
            fin = {}
            for op in self.all_dma:
                fin[id(op.sem)] = (op.sem, max(op.val, fin.get(id(op.sem), (None, 0))[1]))
            finals = list(fin.values())

            def make(e):
                oplist = self.ops[e]

                def body(eng):
                    waited = {}

                    def wait(sem, val):
                        key = id(sem)
                        if waited.get(key, 0) >= val:
                            return
                        waited[key] = val
                        eng.wait_ge(sem, val)

                    for op in oplist:
                        if op.slotwait is not None:
                            wait(*op.slotwait)
                        for d in op.deps:
                            wait(d.sem, d.val)
                        ins = op.fn(eng)
                        if op.signals:
                            ins.then_inc(op.sem, 16 if op.dma else 1)
                    if e == final_wait_eng:
                        for sem, val in finals:
                            wait(sem, val)
                return body

            for e in ENGS:
                if self.ops[e] or e == final_wait_eng:
                    engobj[e](make(e))


class _Rec:
    def __getattr__(self, name):
        def rec(*args, **kw):
            return lambda eng: getattr(eng, name)(*args, **kw)
        return rec


R = _Rec()


def run_rr(gens, offset=0):
    gens = list(gens)
    active = []
    step = 0
    while gens or active:
        if gens and step % max(offset, 1) == 0:
            active.append(gens.pop(0))
        if not offset:
            active.extend(gens)
            gens = []
        for g_ in list(active):
            try:
                next(g_)
            except StopIteration:
                active.remove(g_)
        step += 1


class PhaseProg(Prog):
    def add(self, eng, fn, reads=(), writes=(), dma=False):
        op = super().add(eng, fn, reads, writes, dma)
        self._apply_barrier(op)
        for d in op.deps:
            d.signals = True
        return op


def build_program(nseq=4, nlayers=DEPTH, debug=None):
    nc = bass.Bass("TRN2", target_bir_lowering=False)
    P = PhaseProg(nc)
    NL = nlayers

    def din(name, shape, dt=F32):
        return nc.dram_tensor(name, list(shape), dt, kind="ExternalInput").ap()

    def dscr(name, shape, dt=BF16):
        return nc.dram_tensor(name, list(shape), dt, kind="Internal").ap()

    x_in = din("x", [nseq, L, D])
    out_d = nc.dram_tensor("out", [nseq, L, D], F32, kind="ExternalOutput").ap()
    cst_in = din("cst", [128, 6, 128])
    bb_in = din("bb", [128, 16, 384])
    am_in = din("amask", [128, 384])
    wfm_in = din("wfm", [NL, 22, 128, 8, 128])
    wtm_in = din("wtm", [NL, 128, 8, 1312])
    wout_in = din("wout", [NL, 128, 8, 16, 128])
    wup_in = din("wup", [NL, 44, 128, 8, 128])
    wdn_in = din("wdn", [NL, 128, 8, 22, 128])
    nw1_in = din("nw1", [NL, 128, 8])
    nw2_in = din("nw2", [NL, 128, 8])
    sw_in = din("sw", [NL, 128, 16])
    cw_in = din("cw", [NL, 128, 12, 7])
    cb_in = din("cb", [NL, 128, 12])
    dtb_in = din("dtb", [NL, 128, 32])
    alog_in = din("alog", [NL, 128, 32])
    dsk_in = din("dsk", [NL, 128, 1024])
    sink_in = din("sink", [NL, 128, 16])
    fcw_in = din("fcw", [NL, 128, 22, 3])
    fcb_in = din("fcb", [NL, 128, 22])
    fw_in = din("fw", [128, 8])

    wfm_s = dscr("wfm_s", [NL, 22, 128, 8, 128])
    wtm_s = dscr("wtm_s", [NL, 128, 8, 1312])
    wout_s = dscr("wout_s", [NL, 128, 8, 16, 128])
    wup_s = dscr("wup_s", [NL, 44, 128, 8, 128])
    wdn_s = dscr("wdn_s", [NL, 128, 8, 22, 128])
    xs_d = dscr("xs_d", [L, 1024])
    btok_d = dscr("btok_d", [L, 256])
    bT_d = dscr("bT_d", [2, 128, L])
    cT_d = dscr("cT_d", [2, 128, L])
    z_d = dscr("z_d", [L, 1024])
    v_d = dscr("v_d", [L, 256])
    qT_d = dscr("qT_d", [1024, L])
    kT_d = dscr("kT_d", [256, L])
    mixT_d = dscr("mixT_d", [2048, L])
    aT_d = dscr("aT_d", [DFF, L])
    B_wscr = P.buf("wscr")
    B_xs_d, B_btok_d, B_bT_d, B_cT_d, B_z_d, B_v_d, B_qT_d, B_kT_d, B_mixT_d, B_aT_d = P.bufs(10, "scr")

    with contextlib.ExitStack() as top:
        uniq = [0]

        def sbt(stack, name, shape, dt):
            uniq[0] += 1
            return stack.enter_context(nc.sbuf_tensor(f"{name}_{uniq[0]}", list(shape), dt))

        def pst(stack, name, shape, dt):
            uniq[0] += 1
            return stack.enter_context(nc.psum_tensor(f"{name}_{uniq[0]}", list(shape), dt))

        xT = sbt(top, "xT", [128, 8, L], F32)
        B_xT = P.bufs(4, "xT")
        cstf = sbt(top, "cstf", [128, 6, 128], F32)
        cstb = sbt(top, "cstb", [128, 6, 128], BF16)
        fwt = sbt(top, "fwt", [128, 8], F32)
        epsc = sbt(top, "epsc", [128, 1], F32)
        onec = sbt(top, "onec", [128, 1], F32)
        B_cst = P.buf("cst")
        P.dma("sp", cstf[:], cst_in, writes=[B_cst])
        P.dma("sp", fwt[:], fw_in, writes=[B_cst])
        P.add("dve", R.tensor_copy(out=cstb[:], in_=cstf[:]), [B_cst], [B_cst])
        P.add("dve", R.memset(epsc[:], EPS), [], [B_cst])
        P.add("dve", R.memset(onec[:], 1.0), [], [B_cst])
        identf = cstf[:, 0, :]
        identb = cstb[:, 0, :]
        onesb = cstb[:, 5, :]

        with contextlib.ExitStack() as ph:
            CH = 8192
            stg = [sbt(ph, f"wst{i}", [128, CH], F32) for i in range(2)]
            obf = [sbt(ph, f"wob{i}", [128, CH], BF16) for i in range(2)]
            nwt = sbt(ph, "nwt", [128, NL, 2, 8], F32)
            swt = sbt(ph, "swt", [128, NL, 16], F32)
            B_stg = P.bufs(2, "wst")
            B_obf = P.bufs(2, "wob")
            B_nw = P.buf("nw")
            for l in range(NL):
                P.dma("sp", nwt[:, l, 0, :], nw1_in[l], writes=[B_nw])
                P.dma("sp", nwt[:, l, 1, :], nw2_in[l], writes=[B_nw])
                P.dma("sp", swt[:, l, :], sw_in[l], writes=[B_nw])
            cnt = [0]

            def cast_piece(src, dst, shape, scale_ap):
                i = cnt[0] % 2
                cnt[0] += 1
                n = int(np.prod(shape[1:]))
                pat = {2: "p (a) -> p a", 3: "p (a b) -> p a b", 4: "p (a b c) -> p a b c"}[len(shape)]
                kw = {k: v for k, v in zip("abc", shape[1:])}
                sv = stg[i][:, 0:n].rearrange(pat, **kw) if len(shape) > 2 else stg[i][:, 0:n]
                ov = obf[i][:, 0:n].rearrange(pat, **kw) if len(shape) > 2 else obf[i][:, 0:n]
                P.dma("sp", sv, src, writes=[B_stg[i]])
                eng = "dve" if cnt[0] % 3 else "pool"
                if scale_ap is None:
                    P.add("act", R.activation(out=ov, in_=sv, func=AF.Copy), [B_stg[i]], [B_obf[i]])
                else:
                    P.add(eng, R.tensor_tensor(out=ov, in0=sv, in1=scale_ap, op=ALU.mult),
                          [B_stg[i], B_nw], [B_obf[i]])
                P.dma("sp", dst, ov, reads=[B_obf[i]], writes=[B_wscr])

            for l in range(NL):
                n1 = nwt[:, l, 0, :]
                n2 = nwt[:, l, 1, :]
                for a in range(0, 22, 8):
                    na = min(8, 22 - a)
                    cast_piece(wfm_in[l, a:a + na].rearrange("a p k c -> p a k c"),
                               wfm_s[l, a:a + na].rearrange("a p k c -> p a k c"), [128, na, 8, 128],
                               n1.unsqueeze(1).unsqueeze(3).to_broadcast([128, na, 8, 128]))
                for k0 in range(0, 8, 4):
                    cast_piece(wtm_in[l, :, k0:k0 + 4, :], wtm_s[l, :, k0:k0 + 4, :], [128, 4, 1312],
                               n1[:, k0:k0 + 4].unsqueeze(2).to_broadcast([128, 4, 1312]))
                for o in range(0, 8, 4):
                    cast_piece(wout_in[l, :, o:o + 4], wout_s[l, :, o:o + 4], [128, 4, 16, 128],
                               swt[:, l, :].unsqueeze(1).unsqueeze(3).to_broadcast([128, 4, 16, 128]))
                for a in range(0, 44, 8):
                    na = min(8, 44 - a)
                    cast_piece(wup_in[l, a:a + na].rearrange("a p k c -> p a k c"),
                               wup_s[l, a:a + na].rearrange("a p k c -> p a k c"), [128, na, 8, 128],
                               n2.unsqueeze(1).unsqueeze(3).to_broadcast([128, na, 8, 128]))
                for o in range(0, 8, 2):
                    cast_piece(wdn_in[l, :, o:o + 2], wdn_s[l, :, o:o + 2], [128, 2, 22, 128], None)
            P.barrier()

        biasm_d = dscr("biasm_d", [128, 16, 384], BF16)
        B_biasm_d = P.buf("biasm_d")
        with contextlib.ExitStack() as ph:
            bm0 = sbt(ph, "bm0", [128, 16, 384], F32)
            amt = sbt(ph, "amt", [128, 384], F32)
            B_bm0 = P.buf("bm0")
            P.dma("sp", bm0[:], bb_in, writes=[B_bm0])
            P.dma("sp", amt[:], am_in, writes=[B_bm0])
            bm1 = sbt(ph, "bm1", [128, 16, 384], BF16)
            P.add("dve", R.tensor_tensor(out=bm0[:], in0=bm0[:],
                                         in1=amt[:].unsqueeze(1).to_broadcast([128, 16, 384]),
                                         op=ALU.add), [B_bm0], [B_bm0])
            P.add("act", R.activation(out=bm1[:].rearrange("p a b -> p (a b)"), in_=bm0[:].rearrange("p a b -> p (a b)"), func=AF.Exp),
                  [B_bm0], [B_bm0])
            P.dma("sp", biasm_d, bm1[:], reads=[B_bm0], writes=[B_biasm_d])
            P.barrier()

        def rmsnorm_to(ph, dst_fn, Bdst, tag):
            sq = [sbt(ph, f"sq{tag}{i}", [128, 8, 512], BF16) for i in range(2)]
            rs = [sbt(ph, f"rs{tag}{i}", [128, 512], F32) for i in range(2)]
            pn = [pst(ph, f"pn{tag}{i}", [128, 512], F32) for i in range(2)]
            B_sq = P.bufs(2, "sq")
            B_rs = P.bufs(2, "rs")
            B_pn = P.bufs(2, "pn")
            for t in range(4):
                i = t % 2
                ts = slice(512 * t, 512 * t + 512)
                P.add("act", R.activation(out=sq[i][:], in_=xT[:, :, ts], func=AF.Square),
                      [B_xT[t]], [B_sq[i]])
                for k in range(8):
                    P.add("pe", R.matmul(out=pn[i][:], lhsT=onesb, rhs=sq[i][:, k, :],
                                                             start=(k == 0), stop=(k == 7)),
                          [B_sq[i], B_cst], [B_pn[i]])
                P.add("act", R.activation(out=rs[i][:], in_=pn[i][:], func=AF.Ln,
                                          scale=1.0 / D, bias=epsc[:]), [B_pn[i], B_cst], [B_rs[i]])
                P.add("act", R.activation(out=rs[i][:], in_=rs[i][:], func=AF.Exp, scale=-0.5), [B_rs[i]], [B_rs[i]])
                dst_fn(t, ts, rs[i], B_rs[i])

        for s in range(nseq):
            with contextlib.ExitStack() as ph:
                xst = [sbt(ph, f"xst{i}", [128, D], F32) for i in range(2)]
                pl = [pst(ph, f"pl{i}", [128, 8, 128], F32) for i in range(2)]
                B_xst = P.bufs(2, "xst")
                B_pl = P.bufs(2, "pl")
                for i in range(16):
                    j = i % 2
                    P.dma("sp", xst[j][:], x_in[s, 128 * i:128 * i + 128, :], writes=[B_xst[j]])
                    for k in range(8):
                        P.add("pe", R.transpose(out=pl[j][:, k, :], in_=xst[j][:, 128 * k:128 * k + 128],
                                                                    identity=identf), [B_xst[j], B_cst], [B_pl[j]])
                    eng = "act" if i % 2 else "dve"
                    if eng == "act":
                        P.add("act", R.activation(out=xT[:, :, 128 * i:128 * i + 128], in_=pl[j][:], func=AF.Copy),
                              [B_pl[j]], [B_xT[i // 4]])
                    else:
                        P.add("dve", R.tensor_copy(out=xT[:, :, 128 * i:128 * i + 128], in_=pl[j][:]),
                              [B_pl[j]], [B_xT[i // 4]])
                P.barrier()

            for l in range(NL):
                if debug == "loadstore":
                    break
                with contextlib.ExitStack() as lay:
                    dt_all = sbt(lay, "dt_all", [128, 16, 32], F32)
                    adt = sbt(lay, "adt", [128, 16, 32], F32)
                    prm = sbt(lay, "prm", [128, 12 * 7 + 12 + 32 + 32 + 16 + 22 * 3 + 22], F32)
                    dsk = sbt(lay, "dsk", [128, 1024], F32)
                    B_prm = P.buf("prm")
                    B_dt = P.buf("dt")
                    o = 0
                    cwt = prm[:, o:o + 84].rearrange("p (a b) -> p a b", a=12); o += 84
                    cbt = prm[:, o:o + 12]; o += 12
                    dtbt = prm[:, o:o + 32]; o += 32
                    alt = prm[:, o:o + 32]; o += 32
                    sinkt = prm[:, o:o + 16]; o += 16
                    fcwt = prm[:, o:o + 66].rearrange("p (a b) -> p a b", a=22); o += 66
                    fcbt = prm[:, o:o + 22]; o += 22
                    P.dma("sp", cwt, cw_in[l], writes=[B_prm])
                    P.dma("sp", cbt, cb_in[l], writes=[B_prm])
                    P.dma("sp", dtbt, dtb_in[l], writes=[B_prm])
                    P.dma("sp", alt, alog_in[l], writes=[B_prm])
                    P.dma("sp", sinkt, sink_in[l], writes=[B_prm])
                    P.dma("sp", fcwt, fcw_in[l], writes=[B_prm])
                    P.dma("sp", fcbt, fcb_in[l], writes=[B_prm])
                    P.dma("sp", dsk[:], dsk_in[l], writes=[B_prm])
                    P.add("act", R.activation(out=alt, in_=alt, func=AF.Exp), [B_prm], [B_prm])
                    P.add("dve", R.tensor_scalar(out=alt, in0=alt, scalar1=-1.0, scalar2=None, op0=ALU.mult),
                          [B_prm], [B_prm])

                    hsc = contextlib.ExitStack()
                    hT = sbt(hsc, "hT", [128, 8, L], BF16)
                    B_hT = P.bufs(4, "hT")
                    wt = sbt(hsc, "wt", [128, 8, 1312], BF16)
                    B_wt = P.buf("wt")
                    P.dma("sp", wt[:], wtm_s[l], reads=[B_wscr], writes=[B_wt])
                    with contextlib.ExitStack() as ph:
                        def to_h(t, ts, rs, Brs):
                            P.add("dve", R.tensor_tensor(out=hT[:, :, ts], in0=xT[:, :, ts],
                                                                   in1=rs[:].unsqueeze(1).to_broadcast([128, 8, 512]),
                                                                   op=ALU.mult), [B_xT[t], Brs], [B_hT[t]])
                        rmsnorm_to(ph, to_h, B_hT, "a")
                        P.barrier()

                    with contextlib.ExitStack() as ph:
                        wch = [sbt(ph, f"wch{i}", [128, 8, 128], BF16) for i in range(3)]
                        pre = [sbt(ph, f"pre{i}", [128, L + 6], BF16) for i in range(2)]
                        post = [sbt(ph, f"post{i}", [128, L], BF16) for i in range(2)]
                        dg = [sbt(ph, f"dg{i}", [128, 7, 128], BF16) for i in range(2)]
                        tst = [sbt(ph, f"tst{i}", [128, 16, 128], BF16) for i in range(2)]
                        pu = [pst(ph, f"pu{i}", [128, 2, 512], F32) for i in range(2)]
                        pc = pst(ph, "pc", [128, 2, 512], F32)
                        ptr = [pst(ph, f"ptr{i}", [128, 8, 128], BF16) for i in range(2)]
                        B_wch = P.bufs(3, "wch")
                        B_pre = [P.bufs(2, f"pre{i}") for i in range(2)]
                        B_post = [P.bufs(2, f"post{i}") for i in range(2)]
                        B_dg = P.bufs(2, "dg")
                        B_tst = P.bufs(2, "tst")
                        B_pu = P.bufs(2, "pu")
                        B_pc = P.buf("pc")
                        B_ptr = P.bufs(2, "ptr")
                        for i in range(2):
                            P.add("pool", R.memset(pre[i][:], 0.0), [], B_pre[i])
                        for j0 in range(2):
                            P.dma("sp", wch[j0][:], wfm_s[l, j0], reads=[B_wscr], writes=[B_wch[j0]])

                        def st_mm(j, h):
                            wi, pi = j % 3, j % 2
                            for tt in range(2):
                                t = 2 * h + tt
                                for k in range(8):
                                    P.add("pe", R.matmul(out=pu[h][:, tt, :], lhsT=wch[wi][:, k, :], rhs=hT[:, k, 512 * t:512 * t + 512],
                                                         start=(k == 0), stop=(k == 7)), [B_wch[wi], B_hT[t]], [B_pu[h]])
                            hs = slice(1024 * h, 1024 * h + 1024)
                            src = pu[h][:].rearrange("p a b -> p (a b)")
                            if j < 12:
                                P.add("act", R.activation(out=pre[pi][:, 3 + 1024 * h:3 + 1024 * h + 1024], in_=src, func=AF.Copy),
                                      [B_pu[h]], [B_pre[pi][h]])
                            elif j < 20:
                                P.add("act", R.activation(out=post[pi][:, hs], in_=src, func=AF.Copy, scale=0.125), [B_pu[h]], [B_post[pi][h]])
                            else:
                                P.add("act", R.activation(out=post[pi][:, hs], in_=src, func=AF.Copy), [B_pu[h]], [B_post[pi][h]])

                        def st_conv(j, h):
                            if j < 0 or j >= 12:
                                return
                            pi = j % 2
                            if h == 0:
                                P.add("dve", R.tensor_tensor(out=dg[pi][:], in0=identb.unsqueeze(1).to_broadcast([128, 7, 128]),
                                                             in1=cwt[:, j, :].unsqueeze(2).to_broadcast([128, 7, 128]), op=ALU.mult),
                                      [B_cst, B_prm], [B_dg[pi]])
                            for tt in range(2):
                                t = 2 * h + tt
                                for tap in range(7):
                                    P.add("pe", R.matmul(out=pc[:, tt, :], lhsT=dg[pi][:, tap, :], rhs=pre[pi][:, 512 * t + tap:512 * t + tap + 512],
                                                         start=(tap == 0), stop=(tap == 6)), [B_dg[pi]] + B_pre[pi], [B_pc])
                            hs = slice(1024 * h, 1024 * h + 1024)
                            P.add("act", R.activation(out=post[pi][:, hs], in_=pc[:].rearrange("p a b -> p (a b)"), func=AF.Silu,
                                                      bias=cbt[:, j:j + 1]), [B_pc, B_prm], [B_post[pi][h]])

                        def st_tr(j, h):
                            if j < 0 or j >= 10:
                                return
                            pi, ti = j % 2, j % 2
                            for i8 in range(8):
                                i = 8 * h + i8
                                P.add("pe", R.transpose(out=ptr[h][:, i8, :], in_=post[pi][:, 128 * i:128 * i + 128], identity=identb),
                                      [B_post[pi][h], B_cst], [B_ptr[h]])
                            P.add("dve", R.tensor_copy(out=tst[ti][:, 8 * h:8 * h + 8, :], in_=ptr[h][:]), [B_ptr[h]], [B_tst[ti]])

                        def st_store(j):
                            if j < 0:
                                return
                            pi, ti = j % 2, j % 2
                            if j < 8:
                                P.dma("sp", xs_d.rearrange("(i p) c -> p i c", p=128)[:, :, 128 * j:128 * j + 128], tst[ti][:],
                                      reads=[B_tst[ti]], writes=[B_xs_d])
                            elif j < 10:
                                g = j - 8
                                P.dma("sp", btok_d.rearrange("(i p) c -> p i c", p=128)[:, :, 128 * g:128 * g + 128], tst[ti][:],
                                      reads=[B_tst[ti]], writes=[B_btok_d])
                                P.dma("sp", bT_d[g], post[pi][:], reads=B_post[pi], writes=[B_bT_d])
                            elif j < 12:
                                P.dma("sp", cT_d[j - 10], post[pi][:], reads=B_post[pi], writes=[B_cT_d])
                            elif j < 20:
                                P.dma("sp", qT_d[128 * (j - 12):128 * (j - 12) + 128, :], post[pi][:], reads=B_post[pi], writes=[B_qT_d])
                            else:
                                P.dma("sp", kT_d[128 * (j - 20):128 * (j - 20) + 128, :], post[pi][:], reads=B_post[pi], writes=[B_kT_d])

                        for j in range(23):
                            if j + 2 < 22:
                                P.dma("sp", wch[(j + 2) % 3][:], wfm_s[l, j + 2], reads=[B_wscr], writes=[B_wch[(j + 2) % 3]])
                            if j < 22:
                                st_mm(j, 0)
                            st_conv(j - 1, 0)
                            st_tr(j - 2, 1)
                            if j - 2 < 10:
                                st_store(j - 2)
                            if j < 22:
                                st_mm(j, 1)
                            st_conv(j - 1, 1)
                            if j - 1 in (10, 11):
                                st_store(j - 1)
                            st_tr(j - 1, 0)
                            if 12 <= j < 22:
                                st_store(j)
                        P.barrier()

                    with contextlib.ExitStack() as ph:
                        zst = [sbt(ph, f"zst{i}", [128, 1024], BF16) for i in range(2)]
                        vst = [sbt(ph, f"vst{i}", [128, 256], BF16) for i in range(2)]
                        pz = [pst(ph, f"pz{i}", [128, 3, 512], F32) for i in range(2)]
                        B_zst = P.bufs(2, "zst")
                        B_vst = P.bufs(2, "vst")
                        B_pz = P.bufs(2, "pz")
                        for i in range(16):
                            j = i % 2
                            tsl = slice(128 * i, 128 * i + 128)
                            for (bk, c0, n) in ((0, 0, 512), (1, 512, 512), (2, 1024, 288)):
                                for k in range(8):
                                    P.add("pe", R.matmul(
                                        out=pz[j][:, bk, 0:n], lhsT=hT[:, k, tsl], rhs=wt[:, k, c0:c0 + n],
                                        start=(k == 0), stop=(k == 7)), [B_wt, B_hT[i // 4]], [B_pz[j]])
                            P.add("act", R.activation(out=zst[j][:].rearrange("p (a b) -> p a b", a=2),
                                                                     in_=pz[j][:, 0:2, :], func=AF.Silu), [B_pz[j]], [B_zst[j]])
                            P.add("dve", R.tensor_copy(out=vst[j][:], in_=pz[j][:, 2, 0:256]), [B_pz[j]], [B_vst[j]])
                            P.add("dve", R.tensor_tensor(out=dt_all[:, i, :], in0=pz[j][:, 2, 256:288], in1=dtbt,
                                                                            op=ALU.add), [B_pz[j], B_prm], [B_dt])
                            P.dma("sp", z_d[tsl, :], zst[j][:], reads=[B_zst[j]], writes=[B_z_d])
                            P.dma("sp", v_d[tsl, :], vst[j][:], reads=[B_vst[j]], writes=[B_v_d])
                        dtf = dt_all[:].rearrange("p a b -> p (a b)")
                        P.add("act", R.activation(out=dtf, in_=dtf, func=AF.Exp), [B_dt], [B_dt])
                        P.add("act", R.activation(out=dtf, in_=dtf, func=AF.Ln, bias=onec[:]), [B_dt, B_cst], [B_dt])
                        P.add("dve", R.tensor_tensor(out=adt[:], in0=dt_all[:],
                                                               in1=alt.unsqueeze(1).to_broadcast([128, 16, 32]), op=ALU.mult),
                              [B_dt, B_prm], [B_dt])
                        P.barrier()

                    hsc.close()
                    if debug == "inproj":
                        break

                    with contextlib.ExitStack() as ph:
                        biasm = sbt(ph, "biasm", [128, 16, 384], BF16)
                        B_biasm = P.buf("biasm")
                        P.dma("sp", biasm[:], biasm_d, reads=[B_biasm_d], writes=[B_biasm])
                        qg = sbt(ph, "qg", [64, 4, L], BF16)
                        kg = sbt(ph, "kg", [64, L], BF16)
                        vg = sbt(ph, "vg", [128, 16, 65], BF16)
                        pP = [sbt(ph, f"pP{i}", [128, 4, 384], BF16) for i in range(4)]
                        pT = [sbt(ph, f"pT{i}", [128, 4, 3, 128], BF16) for i in range(2)]
                        sm = [sbt(ph, f"sm{i}", [128, 16], F32) for i in range(4)]
                        sk4 = sbt(ph, "sk4", [128, 4], F32)
                        ao = [sbt(ph, f"ao{i}", [128, 4, 256], BF16) for i in range(2)]
                        aT = sbt(ph, "aT", [128, 2, L], BF16)
                        psS = pst(ph, "psS", [128, 4, 512], F32)
                        psT = pst(ph, "psT", [128, 4, 4, 128], BF16)
                        psO = pst(ph, "psO", [128, 4, 128], F32)
                        psA = pst(ph, "psA", [128, 2, 512], BF16)
                        B_qkv = P.buf("qkv")
                        B_pP = P.bufs(4, "pP")
                        B_pT = P.bufs(2, "pT")
                        B_sm = P.bufs(4, "sm")
                        B_ao = P.bufs(2, "ao")
                        B_aT = P.buf("aT")
                        B_sk4 = P.buf("sk4")
                        B_psS, B_psT, B_psO, B_psA = P.bufs(4, "psatt")
                        P.add("pool", R.memset(vg[:], 1.0), [], [B_qkv])
                        P.add("dve", R.tensor_reduce(out=sk4[:], in_=sinkt.rearrange("p (g h) -> p g h", g=4), axis=AX.X, op=ALU.max),
                              [B_prm], [B_sk4])

                        def blk(i):
                            lo = max(0, 128 * (i - 1))
                            hi = min(L, 128 * (i + 2))
                            return lo, hi, hi - lo, (hi - lo) // 128, lo - 128 * (i - 1)

                        def stage_a(g, i, n):
                            lo, hi, nk, nkc, boff = blk(i)
                            bi = n % 4
                            m = sm[bi]
                            qs = slice(128 * i, 128 * i + 128)
                            for h in range(4):
                                P.add("pe", R.matmul(out=psS[:, h, 0:nk], lhsT=qg[:, h, qs], rhs=kg[:, lo:hi], start=True, stop=True),
                                      [B_qkv], [B_psS])
                            P.add("dve", R.tensor_reduce(out=m[:, 0:1], in_=psS[:, :, 0:nk], axis=AX.XY, op=ALU.max), [B_psS], [B_sm[bi]])
                            P.add("dve", R.tensor_scalar(out=m[:, 1:2], in0=m[:, 0:1], scalar1=sk4[:, g:g + 1], scalar2=-1.0,
                                                         op0=ALU.max, op1=ALU.mult), [B_sm[bi], B_sk4], [B_sm[bi]])
                            P.add("act", R.activation(out=pP[bi][:, :, 0:nk], in_=psS[:, :, 0:nk], func=AF.Exp, bias=m[:, 1:2]),
                                  [B_psS, B_sm[bi]], [B_pP[bi]])
                            P.add("act", R.activation(out=m[:, 4:8], in_=sinkt[:, 4 * g:4 * g + 4], func=AF.Exp, bias=m[:, 1:2]),
                                  [B_prm, B_sm[bi]], [B_sm[bi]])
                            P.add("pool", R.tensor_tensor(out=pP[bi][:, :, 0:nk], in0=pP[bi][:, :, 0:nk],
                                                          in1=biasm[:, 4 * g:4 * g + 4, boff:boff + nk], op=ALU.mult),
                                  [B_pP[bi], B_biasm], [B_pP[bi]])

                        def stage_b(g, i, n):
                            lo, hi, nk, nkc, boff = blk(i)
                            bi = n % 4
                            ti = n % 2
                            m = sm[bi]
                            for h in range(4):
                                for kc in range(nkc):
                                    P.add("pe", R.transpose(out=psT[:, h, kc, :], in_=pP[bi][:, h, 128 * kc:128 * kc + 128], identity=identb),
                                          [B_pP[bi], B_cst], [B_psT])
                            P.add("dve", R.tensor_copy(out=pT[ti][:, :, 0:nkc, :], in_=psT[:, :, 0:nkc, :]), [B_psT], [B_pT[ti]])
                            for h in range(4):
                                for kc in range(nkc):
                                    P.add("pe", R.matmul(out=psO[:, h, 0:65], lhsT=pT[ti][:, h, kc, :], rhs=vg[:, lo // 128 + kc, :],
                                                         start=(kc == 0), stop=(kc == nkc - 1)), [B_pT[ti], B_qkv], [B_psO])
                            P.add("dve", R.tensor_tensor(out=m[:, 8:12], in0=psO[:, :, 64], in1=m[:, 4:8], op=ALU.add), [B_psO, B_sm[bi]], [B_sm[bi]])
                            P.add("dve", R.reciprocal(out=m[:, 12:16], in_=m[:, 8:12]), [B_sm[bi]], [B_sm[bi]])
                            ai = (i // 4) % 2
                            P.add("dve", R.tensor_tensor(out=ao[ai][:, i % 4, :].rearrange("p (h d) -> p h d", h=4), in0=psO[:, :, 0:64],
                                                         in1=m[:, 12:16].unsqueeze(2).to_broadcast([128, 4, 64]), op=ALU.mult),
                                  [B_psO, B_sm[bi]], [B_ao[ai]])

                        def flush_ao(t4):
                            ai = t4 % 2
                            for ii in range(4):
                                for c2 in range(2):
                                    P.add("pe", R.transpose(out=psA[:, c2, 128 * ii:128 * ii + 128], in_=ao[ai][:, ii, 128 * c2:128 * c2 + 128],
                                                            identity=identb), [B_ao[ai], B_cst], [B_psA])
                            P.add("act", R.activation(out=aT[:, :, 512 * t4:512 * t4 + 512], in_=psA[:], func=AF.Copy), [B_psA], [B_aT])

                        n = 0
                        for g in range(4):
                            P.dma("sp", qg[:], qT_d[256 * g:256 * g + 256, :].rearrange("(h d) t -> d h t", d=64),
                                  reads=[B_qT_d], writes=[B_qkv])
                            P.dma("sp", kg[:], kT_d[64 * g:64 * g + 64, :], reads=[B_kT_d], writes=[B_qkv])
                            P.dma("sp", vg[:, :, 0:64], v_d.rearrange("(i p) c -> p i c", p=128)[:, :, 64 * g:64 * g + 64],
                                  reads=[B_v_d], writes=[B_qkv])
                            stage_a(g, 0, n)
                            for i in range(16):
                                if i + 1 < 16:
                                    stage_a(g, i + 1, n + i + 1)
                                if i % 4 == 1 and i > 1:
                                    flush_ao(i // 4 - 1)
                                stage_b(g, i, n + i)
                            flush_ao(3)
                            n += 16
                            P.dma("sp", mixT_d[1024 + 256 * g:1024 + 256 * g + 256, :].rearrange("(c p) t -> p c t", p=128), aT[:],
                                  reads=[B_aT], writes=[B_mixT_d])
                        P.barrier()

                    if debug == "attn":
                        break

                    ssd_sc = contextlib.ExitStack()
                    Etot = sbt(ssd_sc, "Etot", [128, 16, 32], F32)
                    Ec = sbt(ssd_sc, "Ec", [128, 16, 32], F32)
                    Wc = sbt(ssd_sc, "Wc", [128, 16, 32], F32)
                    B_E = P.buf("E")
                    with contextlib.ExitStack() as pp:
                        p5 = pst(pp, "p5", [128, 5, 512], F32)
                        E5 = sbt(pp, "E5", [128, 4, 16, 32], F32)
                        B_p5 = P.bufs(5, "p5")
                        for mi in range(5):
                            for c in range(16):
                                P.add("pe", R.matmul(out=p5[:, mi, 32 * c:32 * c + 32], lhsT=cstf[:, 1 + mi, :], rhs=adt[:, c, :],
                                                     start=True, stop=True), [B_cst, B_dt], [B_p5[mi]])
                            edst = (E5[:, mi] if mi < 4 else Etot[:]).rearrange("p a b -> p (a b)")
                            P.add("act", R.activation(out=edst, in_=p5[:, mi, :], func=AF.Exp), [B_p5[mi]], [B_E])
                        P.add("dve", R.tensor_copy(out=Ec[:, :, 0:16], in_=E5[:, 0, :, 0:16]), [B_E], [B_E])
                        P.add("dve", R.tensor_copy(out=Ec[:, :, 16:32], in_=E5[:, 1, :, 16:32]), [B_E], [B_E])
                        P.add("dve", R.tensor_tensor(out=Wc[:, :, 0:16], in0=E5[:, 2, :, 0:16], in1=dt_all[:, :, 0:16], op=ALU.mult),
                              [B_E, B_dt], [B_E])
                        P.add("dve", R.tensor_tensor(out=Wc[:, :, 16:32], in0=E5[:, 3, :, 16:32], in1=dt_all[:, :, 16:32], op=ALU.mult),
                              [B_E, B_dt], [B_E])
                        P.barrier()
                    for g in range(2):
                        with contextlib.ExitStack() as ph:
                            ztb = [sbt(ph, f"zt{i}", [128, 512], BF16) for i in range(4)]
                            xsb = [sbt(ph, f"xs{i}", [128, 512], BF16) for i in range(8)]
                            B_zt = P.bufs(4, "zt")
                            B_xs = P.bufs(8, "xs")
                            bTt = sbt(ph, "bTt", [128, L], BF16)
                            cTt = sbt(ph, "cTt", [128, L], BF16)
                            btk = sbt(ph, "btk", [128, 16, 128], BF16)
                            prevb = sbt(ph, "prevb", [128, 16, 512], BF16)
                            prevf = sbt(ph, "prevf", [128, 16, 512], BF16)
                            ssmT = [sbt(ph, f"ssmT{i}", [128, 4, 128], BF16) for i in range(4)]
                            Hst = [sbt(ph, f"Hst{i}", [128, 512], F32) for i in range(2)]
                            xdd = [sbt(ph, f"xdd{i}", [128, 512], BF16) for i in range(2)]
                            Rm = [sbt(ph, f"Rm{i}", [128, 4, 128], BF16) for i in range(8)]
                            dec = [sbt(ph, f"dec{i}", [128, 4, 128], BF16) for i in range(8)]
                            mixm = [sbt(ph, f"mixm{i}", [128, 4, 128], BF16) for i in range(8)]
                            cbm = [sbt(ph, f"cbm{i}", [128, 2, 128], BF16) for i in range(4)]
                            xdt = [sbt(ph, f"xdt{i}", [128, 2, 512], BF16) for i in range(4)]
                            toff = [sbt(ph, f"toff{i}", [128, 2, 512], BF16) for i in range(4)]
                            xsD = [sbt(ph, f"xsD{i}", [128, 512], BF16) for i in range(4)]
                            yg = [sbt(ph, f"yg{i}", [128, 512], F32) for i in range(4)]
                            yn = [sbt(ph, f"yn{i}", [128, 512], BF16) for i in range(4)]
                            stt = [sbt(ph, f"stt{i}", [128, 4], F32) for i in range(4)]
                            B_in = P.buf("ssdin")
                            B_prevb = P.bufs(16, "prevb")
                            B_prevf = P.bufs(16, "prevf")
                            P.dma("sp", bTt[:], bT_d[g], reads=[B_bT_d], writes=[B_in])
                            P.dma("sp", cTt[:], cT_d[g], reads=[B_cT_d], writes=[B_in])
                            P.dma("sp", btk[:], btok_d.rearrange("(i p) c -> p i c", p=128)[:, :, 128 * g:128 * g + 128],
                                  reads=[B_btok_d], writes=[B_in])
                            xcnt = [0]

                            def load_xs(c):
                                k = xcnt[0] % 8
                                xcnt[0] += 1
                                P.dma("sp", xsb[k][:], xs_d[128 * c:128 * c + 128, 512 * g:512 * g + 512], reads=[B_xs_d], writes=[B_xs[k]])
                                return xsb[k], B_xs[k]


                            def hsel(t3, c, d):
                                return t3[:, c, 16 * d + 8 * g:16 * d + 8 * g + 8]

                            with contextlib.ExitStack() as pp:
                                psSt = [pst(pp, f"psSt{i}", [128, 512], F32) for i in range(2)]
                                B_psSt = P.bufs(2, "psSt")
                                B_H = P.bufs(2, "H")
                                B_xdd = P.bufs(2, "xdd")
                                for d in range(2):
                                    P.add("dve", R.memset(Hst[d][:], 0.0), [], [B_H[d]])

                                def chain(d):
                                    H = Hst[d]
                                    prev, Bprev = (prevf, B_prevf) if d == 0 else (prevb, B_prevb)
                                    for step in range(16):
                                        c = step if d == 0 else 15 - step
                                        P.add("act", R.activation(out=prev[:, c, :], in_=H[:], func=AF.Copy), [B_H[d]], [Bprev[c]])
                                        yield
                                        if step == 15:
                                            break
                                        xsc, Bxsc = load_xs(c)
                                        P.add("pool", R.tensor_tensor(out=xdd[d][:].rearrange("p (h q) -> p h q", h=8),
                                                                      in0=xsc[:].rearrange("p (h q) -> p h q", h=8),
                                                                      in1=hsel(Wc, c, d).unsqueeze(2).to_broadcast([128, 8, 64]), op=ALU.mult),
                                              [Bxsc, B_E], [B_xdd[d]])
                                        yield
                                        P.add("pe", R.matmul(out=psSt[d][:], lhsT=btk[:, c, :], rhs=xdd[d][:], start=True, stop=True),
                                              [B_in, B_xdd[d]], [B_psSt[d]])
                                        yield
                                        P.add("dve", R.tensor_tensor(out=H[:].rearrange("p (h q) -> p h q", h=8),
                                                                     in0=H[:].rearrange("p (h q) -> p h q", h=8),
                                                                     in1=hsel(Etot, c, d).unsqueeze(2).to_broadcast([128, 8, 64]), op=ALU.mult),
                                              [B_H[d], B_E], [B_H[d]])
                                        yield
                                        P.add("dve", R.tensor_tensor(out=H[:], in0=H[:], in1=psSt[d][:], op=ALU.add), [B_H[d], B_psSt[d]], [B_H[d]])
                                        yield

                                run_rr([chain(0), chain(1)], offset=2)
                                P.barrier()

                            with contextlib.ExitStack() as pp:
                                psSeg2 = [pst(pp, f"psSeg{i}", [128, 4, 128], F32) for i in range(2)]
                                psSeg = psSeg2 * 2
                                psOff1 = pst(pp, "psOff", [128, 512], F32)
                                psOff = [psOff1] * 4
                                psY = [pst(pp, f"psY{i}", [128, 512], F32) for i in range(4)]
                                psTr_ = pst(pp, "psTr", [128, 4, 128], BF16)
                                B_psSeg = P.bufs(2, "psSeg") * 2
                                B_psOff = P.bufs(1, "psOff") * 4
                                B_psY = P.bufs(4, "psY")
                                B_psCB = B_psY
                                B_psTr = P.bufs(1, "psTr") * 4
                                B_R = P.bufs(8, "R")
                                B_dec = P.bufs(8, "dec")
                                B_mix = P.bufs(8, "mix")
                                B_cbm = P.bufs(4, "cbm")
                                B_xdt = P.bufs(4, "xdt")
                                B_toff = P.bufs(4, "toff")
                                B_xsD = P.bufs(4, "xsD")
                                B_yg = P.bufs(4, "yg")
                                B_yn = P.bufs(4, "yn")
                                B_stt = P.bufs(4, "stt")
                                B_ssmT = P.bufs(4, "ssmT")

                                def psCB(i):
                                    return psY[i][:, 0:128]

                                def psTr(i):
                                    return psTr_[:]

                                def ychunk(c):
                                    i = c % 4
                                    cs = slice(128 * c, 128 * c + 128)
                                    xsc, Bxsc = load_xs(c)
                                    P.dma("sp", ztb[i][:], z_d[128 * c:128 * c + 128, 512 * g:512 * g + 512], reads=[B_z_d], writes=[B_zt[i]])
                                    P.add("pe", R.matmul(out=psCB(i), lhsT=bTt[:, cs], rhs=cTt[:, cs], start=True, stop=True), [B_in], [B_psCB[i]])
                                    yield
                                    P.add("dve", R.tensor_tensor(out=cbm[i][:], in0=psCB(i).unsqueeze(1).to_broadcast([128, 2, 128]),
                                                                 in1=cstb[:, 1:3, :], op=ALU.mult), [B_psCB[i], B_cst], [B_cbm[i]])
                                    yield
                                    P.add("dve", R.tensor_tensor(
                                        out=xdt[i][:].rearrange("p d (h q) -> p d h q", h=8),
                                        in0=xsc[:].rearrange("p (h q) -> p h q", h=8).unsqueeze(1).to_broadcast([128, 2, 8, 64]),
                                        in1=dt_all[:, c, :].rearrange("p (d h) -> p d h", d=2)[:, :, 8 * g:8 * g + 8].unsqueeze(3).to_broadcast([128, 2, 8, 64]),
                                        op=ALU.mult), [Bxsc, B_dt], [B_xdt[i]])
                                    yield
                                    P.add("dve", R.tensor_tensor(out=xsD[i][:], in0=xsc[:], in1=dsk[:, 512 * g:512 * g + 512], op=ALU.mult),
                                          [Bxsc, B_prm], [B_xsD[i]])
                                    yield
                                    P.add("pe", R.matmul(out=psY[i][:], lhsT=identb, rhs=xsD[i][:], start=True, stop=False),
                                          [B_cst, B_xsD[i]], [B_psY[i]])
                                    yield
                                    n4 = 0
                                    for d in range(2):
                                        for hh in range(2):
                                            r = 2 * i + (n4 % 2)
                                            n4 += 1
                                            hs4 = slice(4 * hh, 4 * hh + 4)
                                            P.add("pool", R.tensor_tensor(out=Rm[r][:, 0:4, :],
                                                                          in0=hsel(adt, c, d)[:, hs4].unsqueeze(2).to_broadcast([128, 4, 128]),
                                                                          in1=cstb[:, 1 + d, :].unsqueeze(1).to_broadcast([128, 4, 128]), op=ALU.mult),
                                                  [B_dt, B_cst], [B_R[r]])
                                            yield
                                            P.add("pe", R.matmul(out=psSeg[i][:], lhsT=cstb[:, 3 + d, :], rhs=Rm[r][:, 0:4, :], start=True, stop=True),
                                                  [B_cst, B_R[r]], [B_psSeg[i]])
                                            yield
                                            P.add("act", R.activation(out=dec[r][:, 0:4, :], in_=psSeg[i][:], func=AF.Exp), [B_psSeg[i]], [B_dec[r]])
                                            yield
                                            P.add("dve", R.tensor_tensor(out=mixm[r][:, 0:4, :], in0=dec[r][:, 0:4, :],
                                                                         in1=cbm[i][:, d, :].unsqueeze(1).to_broadcast([128, 4, 128]), op=ALU.mult),
                                                  [B_dec[r], B_cbm[i]], [B_mix[r]])
                                            yield
                                            for h4 in range(4):
                                                h = 4 * hh + h4
                                                P.add("pe", R.matmul(out=psY[i][:, 64 * h:64 * h + 64], lhsT=mixm[r][:, h4, :],
                                                                     rhs=xdt[i][:, d, 64 * h:64 * h + 64], start=False, stop=False),
                                                      [B_mix[r], B_xdt[i]], [B_psY[i]])
                                            yield
                                    for d in range(2):
                                        prev, Bprev = (prevf, B_prevf) if d == 0 else (prevb, B_prevb)
                                        P.add("pe", R.matmul(out=psOff[i][:], lhsT=cTt[:, cs], rhs=prev[:, c, :], start=True, stop=True),
                                              [B_in, Bprev[c]], [B_psOff[i]])
                                        yield
                                        P.add("dve", R.tensor_tensor(
                                            out=toff[i][:, d, :].rearrange("p (h q) -> p h q", h=8),
                                            in0=psOff[i][:].rearrange("p (h q) -> p h q", h=8),
                                            in1=hsel(Ec, c, d).unsqueeze(2).to_broadcast([128, 8, 64]),
                                            op=ALU.mult), [B_psOff[i], B_E], [B_toff[i]])
                                        yield
                                        P.add("pe", R.matmul(out=psY[i][:], lhsT=identb, rhs=toff[i][:, d, :], start=False, stop=(d == 1)),
                                              [B_cst, B_toff[i]], [B_psY[i]])
                                        yield
                                    P.add("dve", R.tensor_tensor(out=yg[i][:], in0=psY[i][:], in1=ztb[i][:], op=ALU.mult),
                                          [B_psY[i], B_zt[i]], [B_yg[i]])
                                    P.add("dve", R.memset(stt[i][:], 0.0), [], [B_stt[i]])
                                    yield
                                    P.add("act", R.activation(out=yn[i][:], in_=yg[i][:], func=AF.Square, accum_out=stt[i][:, 0:1]),
                                          [B_yg[i], B_stt[i]], [B_yn[i], B_stt[i]])
                                    yield
                                    P.add("act", R.activation(out=stt[i][:, 1:2], in_=stt[i][:, 0:1], func=AF.Ln, scale=1.0 / 512, bias=epsc[:]),
                                          [B_stt[i], B_cst], [B_stt[i]])
                                    yield
                                    P.add("act", R.activation(out=stt[i][:, 2:3], in_=stt[i][:, 1:2], func=AF.Exp, scale=-0.5), [B_stt[i]], [B_stt[i]])
                                    yield
                                    P.add("act", R.activation(out=yn[i][:], in_=yg[i][:], func=AF.Copy, scale=stt[i][:, 2:3]),
                                          [B_yg[i], B_stt[i]], [B_yn[i]])
                                    yield
                                    for q4 in range(4):
                                        P.add("pe", R.transpose(out=psTr(i)[:, q4, :], in_=yn[i][:, 128 * q4:128 * q4 + 128], identity=identb),
                                              [B_yn[i], B_cst], [B_psTr[i]])
                                    yield
                                    P.add("dve", R.tensor_copy(out=ssmT[i][:], in_=psTr(i)), [B_psTr[i]], [B_ssmT[i]])
                                    P.dma("sp", mixT_d[512 * g:512 * g + 512, cs].rearrange("(c p) t -> p c t", p=128), ssmT[i][:],
                                          reads=[B_ssmT[i]], writes=[B_mixT_d])
                                    yield

                                def ythread(t):
                                    for c in range(t, 16, 4):
                                        yield from ychunk(c)

                                run_rr([ythread(0), ythread(1), ythread(2), ythread(3)], offset=9)
                                P.barrier()

                    ssd_sc.close()
                    if debug == "ssd":
                        break

                    with contextlib.ExitStack() as ph:
                        wo = sbt(ph, "wo", [128, 8, 16, 128], BF16)
                        mt = [sbt(ph, f"mt{i}", [128, 16, 512], BF16) for i in range(2)]
                        po = [pst(ph, f"po{i}", [128, 512], F32) for i in range(4)]
                        B_wo = P.bufs(8, "wo")
                        B_mt = P.bufs(2, "mt")
                        B_po = P.bufs(4, "po")
                        for oc in range(8):
                            P.dma("sp", wo[:, oc], wout_s[l, :, oc], reads=[B_wscr], writes=[B_wo[oc]])
                        n = 0
                        P.dma("sp", mt[0][:], mixT_d.rearrange("(k p) t -> p k t", p=128)[:, :, 0:512], reads=[B_mixT_d], writes=[B_mt[0]])
                        for t in range(4):
                            i = t % 2
                            ts = slice(512 * t, 512 * t + 512)
                            if t + 1 < 4:
                                P.dma("sp", mt[1 - i][:], mixT_d.rearrange("(k p) t -> p k t", p=128)[:, :, 512 * (t + 1):512 * (t + 2)],
                                      reads=[B_mixT_d], writes=[B_mt[1 - i]])
                            for oc in range(8):
                                b = n % 4
                                n += 1
                                for k in range(16):
                                    P.add("pe", R.matmul(out=po[b][:], lhsT=wo[:, oc, k, :], rhs=mt[i][:, k, :],
                                                                                          start=(k == 0), stop=(k == 15)), [B_wo[oc], B_mt[i]], [B_po[b]])
                                P.add("dve", R.tensor_tensor(out=xT[:, oc, ts], in0=xT[:, oc, ts], in1=po[b][:], op=ALU.add),
                                      [B_po[b], B_xT[t]], [B_xT[t]])
                        P.barrier()

                    if debug == "outproj":
                        break

                    wdsc = contextlib.ExitStack()
                    wd = sbt(wdsc, "wd", [128, 8, 22, 128], BF16)
                    B_wd = P.buf("wd")
                    P.dma("sp", wd[:], wdn_s[l], reads=[B_wscr], writes=[B_wd])
                    hsc = contextlib.ExitStack()
                    hT = sbt(hsc, "hT", [128, 8, L], BF16)
                    B_hT = P.bufs(4, "hT")
                    with contextlib.ExitStack() as ph:
                        def to_h2(t, ts, rs, Brs):
                            P.add("dve", R.tensor_tensor(out=hT[:, :, ts], in0=xT[:, :, ts],
                                                                   in1=rs[:].unsqueeze(1).to_broadcast([128, 8, 512]),
                                                                   op=ALU.mult), [B_xT[t], Brs], [B_hT[t]])
                        rmsnorm_to(ph, to_h2, B_hT, "b")
                        P.barrier()

                    with contextlib.ExitStack() as ph:
                        wg = [sbt(ph, f"wg{i}", [128, 8, 128], BF16) for i in range(2)]
                        wu = [sbt(ph, f"wu{i}", [128, 8, 128], BF16) for i in range(2)]
                        pre = [sbt(ph, f"fpre{i}", [128, L + 2], BF16) for i in range(2)]
                        sg = [sbt(ph, f"sg{i}", [128, L], F32) for i in range(1)]
                        aTt = [sbt(ph, f"aTt{i}", [128, L], BF16) for i in range(2)]
                        dg3 = [sbt(ph, f"dg3{i}", [128, 3, 128], BF16) for i in range(2)]
                        pG = [pst(ph, f"pG{i}", [128, 2, 512], F32) for i in range(2)]
                        pU = [pst(ph, f"pU{i}", [128, 2, 512], F32) for i in range(2)]
                        B_w = P.bufs(2, "wgu")
                        B_pre = [P.bufs(2, f"fpre{i}") for i in range(2)]
                        B_sg = P.bufs(2, "sg")
                        B_aTt = P.bufs(2, "aTt")
                        B_dg3 = P.bufs(2, "dg3")
                        B_pG = P.bufs(2, "pG")
                        B_pU = P.bufs(2, "pU")
                        for i in range(2):
                            P.add("pool", R.memset(pre[i][:], 0.0), [], B_pre[i])
                        P.dma("sp", wg[0][:], wup_s[l, 0], reads=[B_wscr], writes=[B_w[0]])
                        P.dma("sp", wu[0][:], wup_s[l, 22], reads=[B_wscr], writes=[B_w[0]])
                        for j in range(22):
                            i = j % 2
                            if j + 1 < 22:
                                P.dma("sp", wg[1 - i][:], wup_s[l, j + 1], reads=[B_wscr], writes=[B_w[1 - i]])
                                P.dma("sp", wu[1 - i][:], wup_s[l, 22 + j + 1], reads=[B_wscr], writes=[B_w[1 - i]])
                            P.add("dve", R.tensor_tensor(
                                out=dg3[i][:], in0=identb.unsqueeze(1).to_broadcast([128, 3, 128]),
                                in1=fcwt[:, j, :].unsqueeze(2).to_broadcast([128, 3, 128]), op=ALU.mult), [B_cst, B_prm], [B_dg3[i]])
                            for h in range(2):
                                for tt in range(2):
                                    t = 2 * h + tt
                                    for k in range(8):
                                        P.add("pe", R.matmul(
                                            out=pG[h][:, tt, :], lhsT=wg[i][:, k, :], rhs=hT[:, k, 512 * t:512 * t + 512],
                                            start=(k == 0), stop=(k == 7)), [B_w[i], B_hT[t]], [B_pG[h]])
                                P.add("act", R.activation(out=pre[i][:, 1 + 1024 * h:1 + 1024 * h + 1024],
                                                                             in_=pG[h][:].rearrange("p a b -> p (a b)"), func=AF.Copy),
                                      [B_pG[h]], [B_pre[i][h]])
                            for h in range(2):
                                for tt in range(2):
                                    t = 2 * h + tt
                                    for k in range(8):
                                        P.add("pe", R.matmul(
                                            out=pU[h][:, tt, :], lhsT=wu[i][:, k, :], rhs=hT[:, k, 512 * t:512 * t + 512],
                                            start=(k == 0), stop=(k == 7)), [B_w[i], B_hT[t]], [B_pU[h]])
                            for h in range(2):
                                for tt in range(2):
                                    t = 2 * h + tt
                                    for tap in range(3):
                                        P.add("pe", R.matmul(
                                            out=pG[h][:, tt, :], lhsT=dg3[i][:, tap, :], rhs=pre[i][:, 512 * t + tap:512 * t + tap + 512],
                                            start=(tap == 0), stop=(tap == 2)), [B_dg3[i]] + B_pre[i], [B_pG[h]])
                                hs = slice(1024 * h, 1024 * h + 1024)
                                P.add("act", R.activation(out=sg[0][:, hs], in_=pG[h][:].rearrange("p a b -> p (a b)"),
                                                                                   func=AF.Silu, bias=fcbt[:, j:j + 1]), [B_pG[h], B_prm], [B_sg[h]])
                                P.add("dve", R.tensor_tensor(out=aTt[i][:, hs], in0=sg[0][:, hs],
                                                                                         in1=pU[h][:].rearrange("p a b -> p (a b)"), op=ALU.mult),
                                      [B_sg[h], B_pU[h]], [B_aTt[i]])
                            P.dma("sp", aT_d[128 * j:128 * j + 128, :], aTt[i][:], reads=[B_aTt[i]], writes=[B_aT_d])
                        P.barrier()

                    hsc.close()
                    with contextlib.ExitStack() as ph:
                        at = [sbt(ph, f"at{i}", [128, 22, 512], BF16) for i in range(2)]
                        po = [pst(ph, f"pd{i}", [128, 512], F32) for i in range(4)]
                        B_at = P.bufs(2, "at")
                        B_po = P.bufs(4, "pd")
                        n = 0
                        P.dma("sp", at[0][:], aT_d.rearrange("(k p) t -> p k t", p=128)[:, :, 0:512], reads=[B_aT_d], writes=[B_at[0]])
                        for t in range(4):
                            i = t % 2
                            ts = slice(512 * t, 512 * t + 512)
                            if t + 1 < 4:
                                P.dma("sp", at[1 - i][:], aT_d.rearrange("(k p) t -> p k t", p=128)[:, :, 512 * (t + 1):512 * (t + 2)],
                                      reads=[B_aT_d], writes=[B_at[1 - i]])
                            for oc in range(8):
                                b = n % 4
                                n += 1
                                for k in range(22):
                                    P.add("pe", R.matmul(out=po[b][:], lhsT=wd[:, oc, k, :], rhs=at[i][:, k, :],
                                                                                          start=(k == 0), stop=(k == 21)), [B_wd, B_at[i]], [B_po[b]])
                                P.add("dve", R.tensor_tensor(out=xT[:, oc, ts], in0=xT[:, oc, ts], in1=po[b][:], op=ALU.add),
                                      [B_po[b], B_xT[t]], [B_xT[t]])
                        P.barrier()
                    wdsc.close()

            with contextlib.ExitStack() as ph:
                xn = [sbt(ph, f"xn{i}", [128, 8, 512], F32) for i in range(2)]
                ost = [sbt(ph, f"ost{i}", [128, D], F32) for i in range(2)]
                pf = [pst(ph, f"pf{i}", [128, 8, 128], F32) for i in range(2)]
                B_xn = P.bufs(2, "xn")
                B_ost = P.bufs(2, "ost")
                B_pf = P.bufs(2, "pf")
                cnt = [0]

                def fin(t, ts, rs, Brs):
                    i = t % 2
                    for k in range(8):
                        P.add("dve", R.scalar_tensor_tensor(out=xn[i][:, k, :], in0=xT[:, k, ts], scalar=fwt[:, k:k + 1],
                                                                          in1=rs[:], op0=ALU.mult, op1=ALU.mult),
                              [B_xT[t], Brs, B_cst], [B_xn[i]])
                    for q in range(4):
                        j = cnt[0] % 2
                        cnt[0] += 1
                        for k in range(8):
                            P.add("pe", R.transpose(out=pf[j][:, k, :], in_=xn[i][:, k, 128 * q:128 * q + 128],
                                                                            identity=identf), [B_xn[i], B_cst], [B_pf[j]])
                        P.add("act", R.activation(out=ost[j][:], in_=pf[j][:].rearrange("p a b -> p (a b)"), func=AF.Copy),
                              [B_pf[j]], [B_ost[j]])
                        r0 = 512 * t + 128 * q
                        P.dma("sp", out_d[s, r0:r0 + 128, :], ost[j][:], reads=[B_ost[j]])
                if debug in ("loadstore",):
                    def fin_copy():
                        for t in range(4):
                            ts = slice(512 * t, 512 * t + 512)
                            i = t % 2
                            P.add("dve", R.tensor_copy(out=xn[i][:], in_=xT[:, :, ts]), [B_xT[t]], [B_xn[i]])
                            for q in range(4):
                                j = cnt[0] % 2
                                cnt[0] += 1
                                for k in range(8):
                                    P.add("pe", R.transpose(out=pf[j][:, k, :], in_=xn[i][:, k, 128 * q:128 * q + 128],
                                                                                         identity=identf), [B_xn[i], B_cst], [B_pf[j]])
                                P.add("act", R.activation(out=ost[j][:], in_=pf[j][:].rearrange("p a b -> p (a b)"), func=AF.Copy),
                                      [B_pf[j]], [B_ost[j]])
                                r0 = 512 * t + 128 * q
                                P.dma("sp", out_d[s, r0:r0 + 128, :], ost[j][:], reads=[B_ost[j]])
                    fin_copy()
                else:
                    rmsnorm_to(ph, fin, None, "f")
                P.barrier()
        if debug is not None:
            for nm, ap_, b_ in (("xs_d", xs_d, B_xs_d), ("btok_d", btok_d, B_btok_d), ("bT_d", bT_d, B_bT_d), ("cT_d", cT_d, B_cT_d),
                                ("z_d", z_d, B_z_d), ("v_d", v_d, B_v_d), ("qT_d", qT_d, B_qT_d), ("kT_d", kT_d, B_kT_d),
                                ("mixT_d", mixT_d, B_mixT_d), ("aT_d", aT_d, B_aT_d)):
                o_ = nc.dram_tensor("dump_" + nm, list(ap_.shape), BF16, kind="ExternalOutput").ap()
                P.dma("sp", o_, ap_, reads=[b_])
        P.emit()
    return nc


def _t5_bucket_np(rel):
    import math
    half = 16
    max_exact = 8
    ret = np.where(rel > 0, half, 0)
    n = np.abs(rel)
    nf = np.maximum(n, 1).astype(np.float32)
    large = max_exact + (np.log(nf / max_exact) / math.log(128 / max_exact) * (half - max_exact)).astype(np.int32)
    large = np.minimum(large, half - 1)
    return ret + np.where(n < max_exact, n, large)


def prep_inputs(inp, nlayers=DEPTH):
    f = np.float32
    NL = nlayers
    kk = np.arange(128)
    cst = np.zeros((128, 6, 128), f)
    cst[:, 0] = np.eye(128, dtype=f)
    cst[:, 1] = (kk[:, None] <= kk[None, :])
    cst[:, 2] = (kk[:, None] >= kk[None, :])
    cst[:, 3] = (kk[:, None] > kk[None, :])
    cst[:, 4] = (kk[:, None] < kk[None, :])
    cst[:, 5] = 1.0
    rel = np.arange(384)[None, :] - 128 - np.arange(128)[:, None]
    bucket = _t5_bucket_np(rel)
    bb = np.ascontiguousarray(np.asarray(inp["rel_bias"], f)[bucket].transpose(0, 2, 1))
    amask = np.where(np.abs(rel) <= 128, 0.0, -30000.0).astype(f)

    w_in = np.asarray(inp["w_in"], f)[:NL]
    Z0, X0, DT0, Q0, K0, V0 = 0, 1024, 2560, 2592, 3616, 3872
    fm_cols = np.concatenate([np.arange(X0, X0 + 1536), np.arange(Q0, Q0 + 1024), np.arange(K0, K0 + 256)])
    tm_cols = np.concatenate([np.arange(Z0, Z0 + 1024), np.arange(V0, V0 + 256), np.arange(DT0, DT0 + 32)])
    wfm = w_in[:, :, fm_cols].reshape(NL, 8, 128, 22, 128).transpose(0, 3, 2, 1, 4)
    wtm = w_in[:, :, tm_cols].reshape(NL, 8, 128, 1312).transpose(0, 2, 1, 3)
    w_out = np.asarray(inp["w_out"], f)[:NL]
    wout = w_out.reshape(NL, 16, 128, 8, 128).transpose(0, 2, 3, 1, 4)
    w_up = np.asarray(inp["w_up"], f)[:NL]
    wup = w_up.reshape(NL, 8, 128, 44, 128).transpose(0, 3, 2, 1, 4)
    w_dn = np.asarray(inp["w_down"], f)[:NL]
    wdn = w_dn.reshape(NL, 22, 128, 8, 128).transpose(0, 2, 3, 1, 4)

    def pcol(v, nch):
        return np.ascontiguousarray(np.asarray(v, f)[:NL].reshape(NL, nch, 128).transpose(0, 2, 1))

    def bc(v):
        v = np.asarray(v, f)[:NL]
        return np.ascontiguousarray(np.broadcast_to(v[:, None, :], (NL, 128, v.shape[-1])))

    sw = np.concatenate([pcol(inp["ssm_norm_w"], 8), np.ones((NL, 128, 8), f)], axis=2)
    cw = np.ascontiguousarray(np.asarray(inp["conv_w"], f)[:NL].reshape(NL, 7, 12, 128).transpose(0, 3, 2, 1))
    fcw = np.ascontiguousarray(np.asarray(inp["ffn_conv_w"], f)[:NL].reshape(NL, 3, 22, 128).transpose(0, 3, 2, 1))
    shared = {
        "cst": cst, "bb": bb, "amask": amask,
        "wfm": np.ascontiguousarray(wfm), "wtm": np.ascontiguousarray(wtm), "wout": np.ascontiguousarray(wout),
        "wup": np.ascontiguousarray(wup), "wdn": np.ascontiguousarray(wdn),
        "nw1": pcol(inp["norm1_w"], 8), "nw2": pcol(inp["norm2_w"], 8), "sw": np.ascontiguousarray(sw),
        "cw": cw, "cb": pcol(inp["conv_b"], 12),
        "dtb": bc(np.asarray(inp["dt_bias"], f).reshape(-1, 32)), "alog": bc(np.asarray(inp["a_log"], f).reshape(-1, 32)),
        "dsk": bc(np.repeat(np.asarray(inp["d_skip"], f), 64, axis=1)),
        "sink": bc(inp["attn_sink"]),
        "fcw": fcw, "fcb": pcol(inp["ffn_conv_b"], 22),
        "fw": np.ascontiguousarray(np.asarray(inp["final_norm_w"], f).reshape(8, 128).T),
    }
    return shared


_CACHE = {}


def kernel(**inputs):
    x = np.asarray(inputs["x"], np.float32)
    nseq = x.shape[0] // NCORES
    shared = prep_inputs(inputs)
    key = ("full", nseq)
    if key not in _CACHE:
        _CACHE[key] = build_program(nseq=nseq, nlayers=DEPTH)
    nc = _CACHE[key]
    in_maps = []
    for c in range(NCORES):
        m = dict(shared)
        m["x"] = np.ascontiguousarray(x[c * nseq:(c + 1) * nseq])
        in_maps.append(m)
    res = run_bass_kernel_spmd(nc, in_maps, core_ids=list(range(NCORES)))
    return np.concatenate([np.asarray(r["out"], np.float32) for r in res.results], axis=0)
```

```python
import contextlib
import numpy as np
import concourse.bass as bass
import concourse.mybir as mybir
from concourse.bass_utils import run_bass_kernel_spmd

F32 = mybir.dt.float32
BF16 = mybir.dt.bfloat16
AF = mybir.ActivationFunctionType
ALU = mybir.AluOpType
AX = mybir.AxisListType

NCORES = 8
L = 2048
D = 1024
DEPTH = 4
DFF = 2816
EPS = 1e-6

ENGS = ("pe", "act", "dve", "pool", "sp")
NDMA_SEMS = 24


class Buf:
    __slots__ = ("name", "writer", "readers")

    def __init__(self, name):
        self.name = name
        self.writer = None
        self.readers = []


class Op:
    __slots__ = ("eng", "fn", "deps", "dma", "signals", "sem", "val", "slotwait")

    def __init__(self, eng, fn, dma):
        self.eng = eng
        self.fn = fn
        self.dma = dma
        self.deps = []
        self.signals = False
        self.sem = None
        self.val = 0
        self.slotwait = None


class Prog:
    def __init__(self, nc):
        self.nc = nc
        self.ops = {e: [] for e in ENGS}
        self.all_dma = []
        self.nbuf = 0
        self.recent = []

    def buf(self, name=None):
        self.nbuf += 1
        return Buf(name or f"b{self.nbuf}")

    def bufs(self, n, name="b"):
        return [self.buf(f"{name}{i}") for i in range(n)]

    def add(self, eng, fn, reads=(), writes=(), dma=False):
        op = Op(eng, fn, dma)
        deps = []
        for r in reads:
            if r.writer is not None:
                deps.append((r.writer, "raw"))
        for w in writes:
            if w.writer is not None:
                deps.append((w.writer, "waw"))
            lastrd = {}
            for rd in w.readers:
                if rd.dma:
                    deps.append((rd, "war"))
                else:
                    lastrd[rd.eng] = rd
            for rd in lastrd.values():
                deps.append((rd, "war"))
        seen = set()
        for d, kind in deps:
            if d is op or id(d) in seen:
                continue
            if d.eng == eng and not d.dma and not dma:
                if eng == "pe" or kind != "raw":
                    continue
            seen.add(id(d))
            op.deps.append(d)
            d.signals = True
        for r in reads:
            r.readers.append(op)
        for w in writes:
            w.writer = op
            w.readers = []
        self.ops[eng].append(op)
        if dma:
            op.signals = True
            self.all_dma.append(op)
        self.recent.append(op)
        return op

    def dma(self, eng, out, in_, reads=(), writes=()):
        return self.add(eng, R.dma_start(out=out, in_=in_), reads, writes, dma=True)

    def barrier(self):
        lasts = []
        for e in ENGS:
            comp = [o for o in self.ops[e] if not o.dma]
            if comp:
                lasts.append(comp[-1])
        dmas = [o for o in self.recent if o.dma]
        self.recent = []
        bb = self.buf("barrier")
        for o in lasts + dmas:
            o.signals = True
        self._pending_barrier = lasts + dmas
        self._barrier_seen = {e: False for e in ENGS}

    def _apply_barrier(self, op):
        pb = getattr(self, "_pending_barrier", None)
        if pb and not self._barrier_seen[op.eng]:
            self._barrier_seen[op.eng] = True
            have = set(id(d) for d in op.deps)
            for d in pb:
                if id(d) not in have and d is not op:
                    if d.eng == op.eng and not d.dma:
                        continue
                    op.deps.append(d)

    def emit(self, final_wait_eng="sp"):
        nc = self.nc
        with contextlib.ExitStack() as es:
            SEM_MAX = 60000
            nsig = {e: sum(1 for o in self.ops[e] if o.signals and not o.dma) for e in ENGS}
            esem = {e: [es.enter_context(nc.semaphore(f"s_{e}{k}")) for k in range(nsig[e] // SEM_MAX + 1)] for e in ENGS}
            dsem = {}
            for e in ENGS:
                if any(o.dma for o in self.ops[e]):
                    dsem[e] = [es.enter_context(nc.semaphore(f"d_{e}{i}")) for i in range(NDMA_SEMS)]
            for e in ENGS:
                cnt = 0
                nd = 0
                for op in self.ops[e]:
                    if op.dma:
                        k = nd % NDMA_SEMS
                        op.sem = dsem[e][k]
                        op.val = 16 * (nd // NDMA_SEMS + 1)
                        if nd >= NDMA_SEMS:
                            op.slotwait = (op.sem, op.val - 16)
                        nd += 1
                    elif op.signals:
                        op.sem = esem[e][cnt // SEM_MAX]
                        op.val = cnt % SEM_MAX + 1
                        cnt += 1
            block = es.enter_context(nc.Block())
            engobj = {"pe": block.tensor, "act": block.scalar, "dve": block.vector,
                      "pool": block.gpsimd, "sp": block.sync}
            fin = {}
            for op in self.all_dma:
                fin[id(op.sem)] = (op.sem, max(op.val, fin.get(id(op.sem), (None, 0))[1]))
            finals = list(fin.values())

            def make(e):
                oplist = self.ops[e]

                def body(eng):
                    waited = {}

                    def wait(sem, val):
                        key = id(sem)
                        if waited.get(key, 0) >= val:
                            return
                        waited[key] = val
                        eng.wait_ge(sem, val)

                    for op in oplist:
                        if op.slotwait is not None:
                            wait(*op.slotwait)
                        for d in op.deps:
                            wait(d.sem, d.val)
                        ins = op.fn(eng)
                        if op.signals:
                            ins.then_inc(op.sem, 16 if op.dma else 1)
                    if e == final_wait_eng:
                        for sem, val in finals:
                            wait(sem, val)
                return body

            for e in ENGS:
                if self.ops[e] or e == final_wait_eng:
                    engobj[e](make(e))


class _Rec:
    def __getattr__(self, name):
        def rec(*args, **kw):
            return lambda eng: getattr(eng, name)(*args, **kw)
        return rec


R = _Rec()


def run_rr(gens, offset=0):
    gens = list(gens)
    active = []
    step = 0
    while gens or active:
        if gens and step % max(offset, 1) == 0:
            active.append(gens.pop(0))
        if not offset:
            active.extend(gens)
            gens = []
        for g_ in list(active):
            try:
                next(g_)
            except StopIteration:
                active.remove(g_)
        step += 1


class PhaseProg(Prog):
    def add(self, eng, fn, reads=(), writes=(), dma=False):
        op = super().add(eng, fn, reads, writes, dma)
        self._apply_barrier(op)
        for d in op.deps:
            d.signals = True
        return op


def build_program(nseq=4, nlayers=DEPTH, debug=None):
    nc = bass.Bass("TRN2", target_bir_lowering=False)
    P = PhaseProg(nc)
    NL = nlayers

    def din(name, shape, dt=F32):
        return nc.dram_tensor(name, list(shape), dt, kind="ExternalInput").ap()

    def dscr(name, shape, dt=BF16):
        return nc.dram_tensor(name, list(shape), dt, kind="Internal").ap()

    x_in = din("x", [nseq, L, D])
    out_d = nc.dram_tensor("out", [nseq, L, D], F32, kind="ExternalOutput").ap()
    cst_in = din("cst", [128, 6, 128])
    bb_in = din("bb", [128, 16, 384])
    am_in = din("amask", [128, 384])
    wfm_in = din("wfm", [NL, 22, 128, 8, 128])
    wtm_in = din("wtm", [NL, 128, 8, 1312])
    wout_in = din("wout", [NL, 128, 8, 16, 128])
    wup_in = din("wup", [NL, 44, 128, 8, 128])
    wdn_in = din("wdn", [NL, 128, 8, 22, 128])
    nw1_in = din("nw1", [NL, 128, 8])
    nw2_in = din("nw2", [NL, 128, 8])
    sw_in = din("sw", [NL, 128, 16])
    cw_in = din("cw", [NL, 128, 12, 7])
    cb_in = din("cb", [NL, 128, 12])
    dtb_in = din("dtb", [NL, 128, 32])
    alog_in = din("alog", [NL, 128, 32])
    dsk_in = din("dsk", [NL, 128, 1024])
    sink_in = din("sink", [NL, 128, 16])
    fcw_in = din("fcw", [NL, 128, 22, 3])
    fcb_in = din("fcb", [NL, 128, 22])
    fw_in = din("fw", [128, 8])

    wfm_s = dscr("wfm_s", [NL, 22, 128, 8, 128])
    wtm_s = dscr("wtm_s", [NL, 128, 8, 1312])
    wout_s = dscr("wout_s", [NL, 128, 8, 16, 128])
    wup_s = dscr("wup_s", [NL, 44, 128, 8, 128])
    wdn_s = dscr("wdn_s", [NL, 128, 8, 22, 128])
    xs_d = dscr("xs_d", [L, 1024])
    btok_d = dscr("btok_d", [L, 256])
    bT_d = dscr("bT_d", [2, 128, L])
    cT_d = dscr("cT_d", [2, 128, L])
    z_d = dscr("z_d", [L, 1024])
    v_d = dscr("v_d", [L, 256])
    qT_d = dscr("qT_d", [1024, L])
    kT_d = dscr("kT_d", [256, L])
    mixT_d = dscr("mixT_d", [2048, L])
    aT_d = dscr("aT_d", [DFF, L])
    B_wscr = P.buf("wscr")
    B_xs_d, B_btok_d, B_bT_d, B_cT_d, B_z_d, B_v_d, B_qT_d, B_kT_d, B_mixT_d, B_aT_d = P.bufs(10, "scr")

    with contextlib.ExitStack() as top:
        uniq = [0]

        def sbt(stack, name, shape, dt):
            uniq[0] += 1
            return stack.enter_context(nc.sbuf_tensor(f"{name}_{uniq[0]}", list(shape), dt))

        def pst(stack, name, shape, dt):
            uniq[0] += 1
            return stack.enter_context(nc.psum_tensor(f"{name}_{uniq[0]}", list(shape), dt))

        xT = sbt(top, "xT", [128, 8, L], F32)
        B_xT = P.bufs(4, "xT")
        cstf = sbt(top, "cstf", [128, 6, 128], F32)
        cstb = sbt(top, "cstb", [128, 6, 128], BF16)
        fwt = sbt(top, "fwt", [128, 8], F32)
        epsc = sbt(top, "epsc", [128, 1], F32)
        onec = sbt(top, "onec", [128, 1], F32)
        B_cst = P.buf("cst")
        P.dma("sp", cstf[:], cst_in, writes=[B_cst])
        P.dma("sp", fwt[:], fw_in, writes=[B_cst])
        P.add("dve", R.tensor_copy(out=cstb[:], in_=cstf[:]), [B_cst], [B_cst])
        P.add("dve", R.memset(epsc[:], EPS), [], [B_cst])
        P.add("dve", R.memset(onec[:], 1.0), [], [B_cst])
        identf = cstf[:, 0, :]
        identb = cstb[:, 0, :]
        onesb = cstb[:, 5, :]

        with contextlib.ExitStack() as ph:
            CH = 8192
            stg = [sbt(ph, f"wst{i}", [128, CH], F32) for i in range(2)]
            obf = [sbt(ph, f"wob{i}", [128, CH], BF16) for i in range(2)]
            nwt = sbt(ph, "nwt", [128, NL, 2, 8], F32)
            swt = sbt(ph, "swt", [128, NL, 16], F32)
            B_stg = P.bufs(2, "wst")
            B_obf = P.bufs(2, "wob")
            B_nw = P.buf("nw")
            for l in range(NL):
                P.dma("sp", nwt[:, l, 0, :], nw1_in[l], writes=[B_nw])
                P.dma("sp", nwt[:, l, 1, :], nw2_in[l], writes=[B_nw])
                P.dma("sp", swt[:, l, :], sw_in[l], writes=[B_nw])
            cnt = [0]

            def cast_piece(src, dst, shape, scale_ap):
                i = cnt[0] % 2
                cnt[0] += 1
                n = int(np.prod(shape[1:]))
                pat = {2: "p (a) -> p a", 3: "p (a b) -> p a b", 4: "p (a b c) -> p a b c"}[len(shape)]
                kw = {k: v for k, v in zip("abc", shape[1:])}
                sv = stg[i][:, 0:n].rearrange(pat, **kw) if len(shape) > 2 else stg[i][:, 0:n]
                ov = obf[i][:, 0:n].rearrange(pat, **kw) if len(shape) > 2 else obf[i][:, 0:n]
                P.dma("sp", sv, src, writes=[B_stg[i]])
                eng = "dve" if cnt[0] % 3 else "pool"
                if scale_ap is None:
                    P.add("act", R.activation(out=ov, in_=sv, func=AF.Copy), [B_stg[i]], [B_obf[i]])
                else:
                    P.add(eng, R.tensor_tensor(out=ov, in0=sv, in1=scale_ap, op=ALU.mult),
                          [B_stg[i], B_nw], [B_obf[i]])
                P.dma("sp", dst, ov, reads=[B_obf[i]], writes=[B_wscr])

            for l in range(NL):
                n1 = nwt[:, l, 0, :]
                n2 = nwt[:, l, 1, :]
                for a in range(0, 22, 8):
                    na = min(8, 22 - a)
                    cast_piece(wfm_in[l, a:a + na].rearrange("a p k c -> p a k c"),
                               wfm_s[l, a:a + na].rearrange("a p k c -> p a k c"), [128, na, 8, 128],
                               n1.unsqueeze(1).unsqueeze(3).to_broadcast([128, na, 8, 128]))
                for k0 in range(0, 8, 4):
                    cast_piece(wtm_in[l, :, k0:k0 + 4, :], wtm_s[l, :, k0:k0 + 4, :], [128, 4, 1312],
                               n1[:, k0:k0 + 4].unsqueeze(2).to_broadcast([128, 4, 1312]))
                for o in range(0, 8, 4):
                    cast_piece(wout_in[l, :, o:o + 4], wout_s[l, :, o:o + 4], [128, 4, 16, 128],
                               swt[:, l, :].unsqueeze(1).unsqueeze(3).to_broadcast([128, 4, 16, 128]))
                for a in range(0, 44, 8):
                    na = min(8, 44 - a)
                    cast_piece(wup_in[l, a:a + na].rearrange("a p k c -> p a k c"),
                               wup_s[l, a:a + na].rearrange("a p k c -> p a k c"), [128, na, 8, 128],
                               n2.unsqueeze(1).unsqueeze(3).to_broadcast([128, na, 8, 128]))
                for o in range(0, 8, 2):
                    cast_piece(wdn_in[l, :, o:o + 2], wdn_s[l, :, o:o + 2], [128, 2, 22, 128], None)
            P.barrier()

        biasm_d = dscr("biasm_d", [128, 16, 384], BF16)
        B_biasm_d = P.buf("biasm_d")
        with contextlib.ExitStack() as ph:
            bm0 = sbt(ph, "bm0", [128, 16, 384], F32)
            amt = sbt(ph, "amt", [128, 384], F32)
            B_bm0 = P.buf("bm0")
            P.dma("sp", bm0[:], bb_in, writes=[B_bm0])
            P.dma("sp", amt[:], am_in, writes=[B_bm0])
            bm1 = sbt(ph, "bm1", [128, 16, 384], BF16)
            P.add("dve", R.tensor_tensor(out=bm0[:], in0=bm0[:],
                                         in1=amt[:].unsqueeze(1).to_broadcast([128, 16, 384]),
                                         op=ALU.add), [B_bm0], [B_bm0])
            P.add("act", R.activation(out=bm1[:].rearrange("p a b -> p (a b)"), in_=bm0[:].rearrange("p a b -> p (a b)"), func=AF.Exp),
                  [B_bm0], [B_bm0])
            P.dma("sp", biasm_d, bm1[:], reads=[B_bm0], writes=[B_biasm_d])
            P.barrier()

        def rmsnorm_to(ph, dst_fn, Bdst, tag):
            sq = [sbt(ph, f"sq{tag}{i}", [128, 8, 512], BF16) for i in range(2)]
            rs = [sbt(ph, f"rs{tag}{i}", [128, 512], F32) for i in range(2)]
            pn = [pst(ph, f"pn{tag}{i}", [128, 512], F32) for i in range(2)]
            B_sq = P.bufs(2, "sq")
            B_rs = P.bufs(2, "rs")
            B_pn = P.bufs(2, "pn")
            for t in range(4):
                i = t % 2
                ts = slice(512 * t, 512 * t + 512)
                P.add("act", R.activation(out=sq[i][:], in_=xT[:, :, ts], func=AF.Square),
                      [B_xT[t]], [B_sq[i]])
                for k in range(8):
                    P.add("pe", R.matmul(out=pn[i][:], lhsT=onesb, rhs=sq[i][:, k, :],
                                                             start=(k == 0), stop=(k == 7)),
                          [B_sq[i], B_cst], [B_pn[i]])
                P.add("act", R.activation(out=rs[i][:], in_=pn[i][:], func=AF.Ln,
                                          scale=1.0 / D, bias=epsc[:]), [B_pn[i], B_cst], [B_rs[i]])
                P.add("act", R.activation(out=rs[i][:], in_=rs[i][:], func=AF.Exp, scale=-0.5), [B_rs[i]], [B_rs[i]])
                dst_fn(t, ts, rs[i], B_rs[i])

        for s in range(nseq):
            with contextlib.ExitStack() as ph:
                xst = [sbt(ph, f"xst{i}", [128, D], F32) for i in range(2)]
                pl = [pst(ph, f"pl{i}", [128, 8, 128], F32) for i in range(2)]
                B_xst = P.bufs(2, "xst")
                B_pl = P.bufs(2, "pl")
                for i in range(16):
                    j = i % 2
                    P.dma("sp", xst[j][:], x_in[s, 128 * i:128 * i + 128, :], writes=[B_xst[j]])
                    for k in range(8):
                        P.add("pe", R.transpose(out=pl[j][:, k, :], in_=xst[j][:, 128 * k:128 * k + 128],
                                                                    identity=identf), [B_xst[j], B_cst], [B_pl[j]])
                    eng = "act" if i % 2 else "dve"
                    if eng == "act":
                        P.add("act", R.activation(out=xT[:, :, 128 * i:128 * i + 128], in_=pl[j][:], func=AF.Copy),
                              [B_pl[j]], [B_xT[i // 4]])
                    else:
                        P.add("dve", R.tensor_copy(out=xT[:, :, 128 * i:128 * i + 128], in_=pl[j][:]),
                              [B_pl[j]], [B_xT[i // 4]])
                P.barrier()

            for l in range(NL):
                if debug == "loadstore":
                    break
                with contextlib.ExitStack() as lay:
                    dt_all = sbt(lay, "dt_all", [128, 16, 32], F32)
                    adt = sbt(lay, "adt", [128, 16, 32], F32)
                    prm = sbt(lay, "prm", [128, 12 * 7 + 12 + 32 + 32 + 16 + 22 * 3 + 22], F32)
                    dsk = sbt(lay, "dsk", [128, 1024], F32)
                    B_prm = P.buf("prm")
                    B_dt = P.buf("dt")
                    o = 0
                    cwt = prm[:, o:o + 84].rearrange("p (a b) -> p a b", a=12); o += 84
                    cbt = prm[:, o:o + 12]; o += 12
                    dtbt = prm[:, o:o + 32]; o += 32
                    alt = prm[:, o:o + 32]; o += 32
                    sinkt = prm[:, o:o + 16]; o += 16
                    fcwt = prm[:, o:o + 66].rearrange("p (a b) -> p a b", a=22); o += 66
                    fcbt = prm[:, o:o + 22]; o += 22
                    P.dma("sp", cwt, cw_in[l], writes=[B_prm])
                    P.dma("sp", cbt, cb_in[l], writes=[B_prm])
                    P.dma("sp", dtbt, dtb_in[l], writes=[B_prm])
                    P.dma("sp", alt, alog_in[l], writes=[B_prm])
                    P.dma("sp", sinkt, sink_in[l], writes=[B_prm])
                    P.dma("sp", fcwt, fcw_in[l], writes=[B_prm])
                    P.dma("sp", fcbt, fcb_in[l], writes=[B_prm])
                    P.dma("sp", dsk[:], dsk_in[l], writes=[B_prm])
                    P.add("act", R.activation(out=alt, in_=alt, func=AF.Exp), [B_prm], [B_prm])
                    P.add("dve", R.tensor_scalar(out=alt, in0=alt, scalar1=-1.0, scalar2=None, op0=ALU.mult),
                          [B_prm], [B_prm])

                    hsc = contextlib.ExitStack()
                    hT = sbt(hsc, "hT", [128, 8, L], BF16)
                    B_hT = P.bufs(4, "hT")
                    wt = sbt(hsc, "wt", [128, 8, 1312], BF16)
                    B_wt = P.buf("wt")
                    P.dma("sp", wt[:], wtm_s[l], reads=[B_wscr], writes=[B_wt])
                    with contextlib.ExitStack() as ph:
                        def to_h(t, ts, rs, Brs):
                            P.add("dve", R.tensor_tensor(out=hT[:, :, ts], in0=xT[:, :, ts],
                                                                   in1=rs[:].unsqueeze(1).to_broadcast([128, 8, 512]),
                                                                   op=ALU.mult), [B_xT[t], Brs], [B_hT[t]])
                        rmsnorm_to(ph, to_h, B_hT, "a")
                        P.barrier()

                    with contextlib.ExitStack() as ph:
                        wch = [sbt(ph, f"wch{i}", [128, 8, 128], BF16) for i in range(3)]
                        pre = [sbt(ph, f"pre{i}", [128, L + 6], BF16) for i in range(2)]
                        post = [sbt(ph, f"post{i}", [128, L], BF16) for i in range(2)]
                        dg = [sbt(ph, f"dg{i}", [128, 7, 128], BF16) for i in range(2)]
                        tst = [sbt(ph, f"tst{i}", [128, 16, 128], BF16) for i in range(2)]
                        pu = [pst(ph, f"pu{i}", [128, 2, 512], F32) for i in range(2)]
                        pc = pst(ph, "pc", [128, 2, 512], F32)
                        ptr = [pst(ph, f"ptr{i}", [128, 8, 128], BF16) for i in range(2)]
                        B_wch = P.bufs(3, "wch")
                        B_pre = [P.bufs(2, f"pre{i}") for i in range(2)]
                        B_post = [P.bufs(2, f"post{i}") for i in range(2)]
                        B_dg = P.bufs(2, "dg")
                        B_tst = P.bufs(2, "tst")
                        B_pu = P.bufs(2, "pu")
                        B_pc = P.buf("pc")
                        B_ptr = P.bufs(2, "ptr")
                        for i in range(2):
                            P.add("pool", R.memset(pre[i][:], 0.0), [], B_pre[i])
                        for j0 in range(2):
                            P.dma("sp", wch[j0][:], wfm_s[l, j0], reads=[B_wscr], writes=[B_wch[j0]])

                        def st_mm(j, h):
                            wi, pi = j % 3, j % 2
                            for tt in range(2):
                                t = 2 * h + tt
                                for k in range(8):
                                    P.add("pe", R.matmul(out=pu[h][:, tt, :], lhsT=wch[wi][:, k, :], rhs=hT[:, k, 512 * t:512 * t + 512],
                                                         start=(k == 0), stop=(k == 7)), [B_wch[wi], B_hT[t]], [B_pu[h]])
                            hs = slice(1024 * h, 1024 * h + 1024)
                            src = pu[h][:].rearrange("p a b -> p (a b)")
                            if j < 12:
                                P.add("act", R.activation(out=pre[pi][:, 3 + 1024 * h:3 + 1024 * h + 1024], in_=src, func=AF.Copy),
                                      [B_pu[h]], [B_pre[pi][h]])
                            elif j < 20:
                                P.add("act", R.activation(out=post[pi][:, hs], in_=src, func=AF.Copy, scale=0.125), [B_pu[h]], [B_post[pi][h]])
                            else:
                                P.add("act", R.activation(out=post[pi][:, hs], in_=src, func=AF.Copy), [B_pu[h]], [B_post[pi][h]])

                        def st_conv(j, h):
                            if j < 0 or j >= 12:
                                return
                            pi = j % 2
                            if h == 0:
                                P.add("dve", R.tensor_tensor(out=dg[pi][:], in0=identb.unsqueeze(1).to_broadcast([128, 7, 128]),
                                                             in1=cwt[:, j, :].unsqueeze(2).to_broadcast([128, 7, 128]), op=ALU.mult),
                                      [B_cst, B_prm], [B_dg[pi]])
                            for tt in range(2):
                                t = 2 * h + tt
                                for tap in range(7):
                                    P.add("pe", R.matmul(out=pc[:, tt, :], lhsT=dg[pi][:, tap, :], rhs=pre[pi][:, 512 * t + tap:512 * t + tap + 512],
                                                         start=(tap == 0), stop=(tap == 6)), [B_dg[pi]] + B_pre[pi], [B_pc])
                            hs = slice(1024 * h, 1024 * h + 1024)
                            P.add("act", R.activation(out=post[pi][:, hs], in_=pc[:].rearrange("p a b -> p (a b)"), func=AF.Silu,
                                                      bias=cbt[:, j:j + 1]), [B_pc, B_prm], [B_post[pi][h]])

                        def st_tr(j, h):
                            if j < 0 or j >= 10:
                                return
                            pi, ti = j % 2, j % 2
                            for i8 in range(8):
                                i = 8 * h + i8
                                P.add("pe", R.transpose(out=ptr[h][:, i8, :], in_=post[pi][:, 128 * i:128 * i + 128], identity=identb),
                                      [B_post[pi][h], B_cst], [B_ptr[h]])
                            P.add("dve", R.tensor_copy(out=tst[ti][:, 8 * h:8 * h + 8, :], in_=ptr[h][:]), [B_ptr[h]], [B_tst[ti]])

                        def st_store(j):
                            if j < 0:
                                return
                            pi, ti = j % 2, j % 2
                            if j < 8:
                                P.dma("sp", xs_d.rearrange("(i p) c -> p i c", p=128)[:, :, 128 * j:128 * j + 128], tst[ti][:],
                                      reads=[B_tst[ti]], writes=[B_xs_d])
                            elif j < 10:
                                g = j - 8
                                P.dma("sp", btok_d.rearrange("(i p) c -> p i c", p=128)[:, :, 128 * g:128 * g + 128], tst[ti][:],
                                      reads=[B_tst[ti]], writes=[B_btok_d])
                                P.dma("sp", bT_d[g], post[pi][:], reads=B_post[pi], writes=[B_bT_d])
                            elif j < 12:
                                P.dma("sp", cT_d[j - 10], post[pi][:], reads=B_post[pi], writes=[B_cT_d])
                            elif j < 20:
                                P.dma("sp", qT_d[128 * (j - 12):128 * (j - 12) + 128, :], post[pi][:], reads=B_post[pi], writes=[B_qT_d])
                            else:
                                P.dma("sp", kT_d[128 * (j - 20):128 * (j - 20) + 128, :], post[pi][:], reads=B_post[pi], writes=[B_kT_d])

                        for j in range(23):
                            if j + 2 < 22:
                                P.dma("sp", wch[(j + 2) % 3][:], wfm_s[l, j + 2], reads=[B_wscr], writes=[B_wch[(j + 2) % 3]])
                            if j < 22:
                                st_mm(j, 0)
                            st_conv(j - 1, 0)
                            st_tr(j - 2, 1)
                            if j - 2 < 10:
                                st_store(j - 2)
                            if j < 22:
                                st_mm(j, 1)
                            st_conv(j - 1, 1)
                            if j - 1 in (10, 11):
                                st_store(j - 1)
                            st_tr(j - 1, 0)
                            if 12 <= j < 22:
                                st_store(j)
                        P.barrier()

                    with contextlib.ExitStack() as ph:
                        zst = [sbt(ph, f"zst{i}", [128, 1024], BF16) for i in range(2)]
                        vst = [sbt(ph, f"vst{i}", [128, 256], BF16) for i in range(2)]
                        pz = [pst(ph, f"pz{i}", [128, 3, 512], F32) for i in range(2)]
                        B_zst = P.bufs(2, "zst")
                        B_vst = P.bufs(2, "vst")
                        B_pz = P.bufs(2, "pz")
                        for i in range(16):
                            j = i % 2
                            tsl = slice(128 * i, 128 * i + 128)
                            for (bk, c0, n) in ((0, 0, 512), (1, 512, 512), (2, 1024, 288)):
                                for k in range(8):
                                    P.add("pe", R.matmul(
                                        out=pz[j][:, bk, 0:n], lhsT=hT[:, k, tsl], rhs=wt[:, k, c0:c0 + n],
                                        start=(k == 0), stop=(k == 7)), [B_wt, B_hT[i // 4]], [B_pz[j]])
                            P.add("act", R.activation(out=zst[j][:].rearrange("p (a b) -> p a b", a=2),
                                                                     in_=pz[j][:, 0:2, :], func=AF.Silu), [B_pz[j]], [B_zst[j]])
                            P.add("dve", R.tensor_copy(out=vst[j][:], in_=pz[j][:, 2, 0:256]), [B_pz[j]], [B_vst[j]])
                            P.add("dve", R.tensor_tensor(out=dt_all[:, i, :], in0=pz[j][:, 2, 256:288], in1=dtbt,
                                                                            op=ALU.add), [B_pz[j], B_prm], [B_dt])
                            P.dma("sp", z_d[tsl, :], zst[j][:], reads=[B_zst[j]], writes=[B_z_d])
                            P.dma("sp", v_d[tsl, :], vst[j][:], reads=[B_vst[j]], writes=[B_v_d])
                        dtf = dt_all[:].rearrange("p a b -> p (a b)")
                        P.add("act", R.activation(out=dtf, in_=dtf, func=AF.Exp), [B_dt], [B_dt])
                        P.add("act", R.activation(out=dtf, in_=dtf, func=AF.Ln, bias=onec[:]), [B_dt, B_cst], [B_dt])
                        P.add("dve", R.tensor_tensor(out=adt[:], in0=dt_all[:],
                                                               in1=alt.unsqueeze(1).to_broadcast([128, 16, 32]), op=ALU.mult),
                              [B_dt, B_prm], [B_dt])
                        P.barrier()

                    hsc.close()
                    if debug == "inproj":
                        break

                    with contextlib.ExitStack() as ph:
                        biasm = sbt(ph, "biasm", [128, 16, 384], BF16)
                        B_biasm = P.buf("biasm")
                        P.dma("sp", biasm[:], biasm_d, reads=[B_biasm_d], writes=[B_biasm])
                        qg = sbt(ph, "qg", [64, 4, L], BF16)
                        kg = sbt(ph, "kg", [64, L], BF16)
                        vg = sbt(ph, "vg", [128, 16, 65], BF16)
                        pP = [sbt(ph, f"pP{i}", [128, 4, 384], BF16) for i in range(4)]
                        pT = [sbt(ph, f"pT{i}", [128, 4, 3, 128], BF16) for i in range(2)]
                        sm = [sbt(ph, f"sm{i}", [128, 16], F32) for i in range(4)]
                        sk4 = sbt(ph, "sk4", [128, 4], F32)
                        ao = [sbt(ph, f"ao{i}", [128, 4, 256], BF16) for i in range(2)]
                        aT = sbt(ph, "aT", [128, 2, L], BF16)
                        psS = pst(ph, "psS", [128, 4, 512], F32)
                        psT = pst(ph, "psT", [128, 4, 4, 128], BF16)
                        psO = pst(ph, "psO", [128, 4, 128], F32)
                        psA = pst(ph, "psA", [128, 2, 512], BF16)
                        B_qkv = P.buf("qkv")
                        B_pP = P.bufs(4, "pP")
                        B_pT = P.bufs(2, "pT")
                        B_sm = P.bufs(4, "sm")
                        B_ao = P.bufs(2, "ao")
                        B_aT = P.buf("aT")
                        B_sk4 = P.buf("sk4")
                        B_psS, B_psT, B_psO, B_psA = P.bufs(4, "psatt")
                        P.add("pool", R.memset(vg[:], 1.0), [], [B_qkv])
                        P.add("dve", R.tensor_reduce(out=sk4[:], in_=sinkt.rearrange("p (g h) -> p g h", g=4), axis=AX.X, op=ALU.max),
                              [B_prm], [B_sk4])

                        def blk(i):
                            lo = max(0, 128 * (i - 1))
                            hi = min(L, 128 * (i + 2))
                            return lo, hi, hi - lo, (hi - lo) // 128, lo - 128 * (i - 1)

                        def stage_a(g, i, n):
                            lo, hi, nk, nkc, boff = blk(i)
                            bi = n % 4
                            m = sm[bi]
                            qs = slice(128 * i, 128 * i + 128)
                            for h in range(4):
                                P.add("pe", R.matmul(out=psS[:, h, 0:nk], lhsT=qg[:, h, qs], rhs=kg[:, lo:hi], start=True, stop=True),
                                      [B_qkv], [B_psS])
                            P.add("dve", R.tensor_reduce(out=m[:, 0:1], in_=psS[:, :, 0:nk], axis=AX.XY, op=ALU.max), [B_psS], [B_sm[bi]])
                            P.add("dve", R.tensor_scalar(out=m[:, 1:2], in0=m[:, 0:1], scalar1=sk4[:, g:g + 1], scalar2=-1.0,
                                                         op0=ALU.max, op1=ALU.mult), [B_sm[bi], B_sk4], [B_sm[bi]])
                            P.add("act", R.activation(out=pP[bi][:, :, 0:nk], in_=psS[:, :, 0:nk], func=AF.Exp, bias=m[:, 1:2]),
                                  [B_psS, B_sm[bi]], [B_pP[bi]])
                            P.add("act", R.activation(out=m[:, 4:8], in_=sinkt[:, 4 * g:4 * g + 4], func=AF.Exp, bias=m[:, 1:2]),
                                  [B_prm, B_sm[bi]], [B_sm[bi]])
                            P.add("pool", R.tensor_tensor(out=pP[bi][:, :, 0:nk], in0=pP[bi][:, :, 0:nk],
                                                          in1=biasm[:, 4 * g:4 * g + 4, boff:boff + nk], op=ALU.mult),
                                  [B_pP[bi], B_biasm], [B_pP[bi]])

                        def stage_b(g, i, n):
                            lo, hi, nk, nkc, boff = blk(i)
                            bi = n % 4
                            ti = n % 2
                            m = sm[bi]
                            for h in range(4):
                                for kc in range(nkc):
                                    P.add("pe", R.transpose(out=psT[:, h, kc, :], in_=pP[bi][:, h, 128 * kc:128 * kc + 128], identity=identb),
                                          [B_pP[bi], B_cst], [B_psT])
                            P.add("dve", R.tensor_copy(out=pT[ti][:, :, 0:nkc, :], in_=psT[:, :, 0:nkc, :]), [B_psT], [B_pT[ti]])
                            for h in range(4):
                                for kc in range(nkc):
                                    P.add("pe", R.matmul(out=psO[:, h, 0:65], lhsT=pT[ti][:, h, kc, :], rhs=vg[:, lo // 128 + kc, :],
                                                         start=(kc == 0), stop=(kc == nkc - 1)), [B_pT[ti], B_qkv], [B_psO])
                            P.add("dve", R.tensor_tensor(out=m[:, 8:12], in0=psO[:, :, 64], in1=m[:, 4:8], op=ALU.add), [B_psO, B_sm[bi]], [B_sm[bi]])
                            P.add("dve", R.reciprocal(out=m[:, 12:16], in_=m[:, 8:12]), [B_sm[bi]], [B_sm[bi]])
                            ai = (i // 4) % 2
                            P.add("dve", R.tensor_tensor(out=ao[ai][:, i % 4, :].rearrange("p (h d) -> p h d", h=4), in0=psO[:, :, 0:64],
                                                         in1=m[:, 12:16].unsqueeze(2).to_broadcast([128, 4, 64]), op=ALU.mult),
                                  [B_psO, B_sm[bi]], [B_ao[ai]])

                        def flush_ao(t4):
                            ai = t4 % 2
                            for ii in range(4):
                                for c2 in range(2):
                                    P.add("pe", R.transpose(out=psA[:, c2, 128 * ii:128 * ii + 128], in_=ao[ai][:, ii, 128 * c2:128 * c2 + 128],
                                                            identity=identb), [B_ao[ai], B_cst], [B_psA])
                            P.add("act", R.activation(out=aT[:, :, 512 * t4:512 * t4 + 512], in_=psA[:], func=AF.Copy), [B_psA], [B_aT])

                        n = 0
                        for g in range(4):
                            P.dma("sp", qg[:], qT_d[256 * g:256 * g + 256, :].rearrange("(h d) t -> d h t", d=64),
                                  reads=[B_qT_d], writes=[B_qkv])
                            P.dma("sp", kg[:], kT_d[64 * g:64 * g + 64, :], reads=[B_kT_d], writes=[B_qkv])
                            P.dma("sp", vg[:, :, 0:64], v_d.rearrange("(i p) c -> p i c", p=128)[:, :, 64 * g:64 * g + 64],
                                  reads=[B_v_d], writes=[B_qkv])
                            stage_a(g, 0, n)
                            for i in range(16):
                                if i + 1 < 16:
                                    stage_a(g, i + 1, n + i + 1)
                                if i % 4 == 1 and i > 1:
                                    flush_ao(i // 4 - 1)
                                stage_b(g, i, n + i)
                            flush_ao(3)
                            n += 16
                            P.dma("sp", mixT_d[1024 + 256 * g:1024 + 256 * g + 256, :].rearrange("(c p) t -> p c t", p=128), aT[:],
                                  reads=[B_aT], writes=[B_mixT_d])
                        P.barrier()

                    if debug == "attn":
                        break

                    ssd_sc = contextlib.ExitStack()
                    Etot = sbt(ssd_sc, "Etot", [128, 16, 32], F32)
                    Ec = sbt(ssd_sc, "Ec", [128, 16, 32], F32)
                    Wc = sbt(ssd_sc, "Wc", [128, 16, 32], F32)
                    B_E = P.buf("E")
                    with contextlib.ExitStack() as pp:
                        p5 = pst(pp, "p5", [128, 5, 512], F32)
                        E5 = sbt(pp, "E5", [128, 4, 16, 32], F32)
                        B_p5 = P.bufs(5, "p5")
                        for mi in range(5):
                            for c in range(16):
                                P.add("pe", R.matmul(out=p5[:, mi, 32 * c:32 * c + 32], lhsT=cstf[:, 1 + mi, :], rhs=adt[:, c, :],
                                                     start=True, stop=True), [B_cst, B_dt], [B_p5[mi]])
                            edst = (E5[:, mi] if mi < 4 else Etot[:]).rearrange("p a b -> p (a b)")
                            P.add("act", R.activation(out=edst, in_=p5[:, mi, :], func=AF.Exp), [B_p5[mi]], [B_E])
                        P.add("dve", R.tensor_copy(out=Ec[:, :, 0:16], in_=E5[:, 0, :, 0:16]), [B_E], [B_E])
                        P.add("dve", R.tensor_copy(out=Ec[:, :, 16:32], in_=E5[:, 1, :, 16:32]), [B_E], [B_E])
                        P.add("dve", R.tensor_tensor(out=Wc[:, :, 0:16], in0=E5[:, 2, :, 0:16], in1=dt_all[:, :, 0:16], op=ALU.mult),
                              [B_E, B_dt], [B_E])
                        P.add("dve", R.tensor_tensor(out=Wc[:, :, 16:32], in0=E5[:, 3, :, 16:32], in1=dt_all[:, :, 16:32], op=ALU.mult),
                              [B_E, B_dt], [B_E])
                        P.barrier()
                    for g in range(2):
                        with contextlib.ExitStack() as ph:
                            ztb = [sbt(ph, f"zt{i}", [128, 512], BF16) for i in range(4)]
                            xsb = [sbt(ph, f"xs{i}", [128, 512], BF16) for i in range(8)]
                            B_zt = P.bufs(4, "zt")
                            B_xs = P.bufs(8, "xs")
                            bTt = sbt(ph, "bTt", [128, L], BF16)
                            cTt = sbt(ph, "cTt", [128, L], BF16)
                            btk = sbt(ph, "btk", [128, 16, 128], BF16)
                            prevb = sbt(ph, "prevb", [128, 16, 512], BF16)
                            prevf = sbt(ph, "prevf", [128, 16, 512], BF16)
                            ssmT = [sbt(ph, f"ssmT{i}", [128, 4, 128], BF16) for i in range(4)]
                            Hst = [sbt(ph, f"Hst{i}", [128, 512], F32) for i in range(2)]
                            xdd = [sbt(ph, f"xdd{i}", [128, 512], BF16) for i in range(2)]
                            Rm = [sbt(ph, f"Rm{i}", [128, 4, 128], BF16) for i in range(8)]
                            dec = [sbt(ph, f"dec{i}", [128, 4, 128], BF16) for i in range(8)]
                            mixm = [sbt(ph, f"mixm{i}", [128, 4, 128], BF16) for i in range(8)]
                            cbm = [sbt(ph, f"cbm{i}", [128, 2, 128], BF16) for i in range(4)]
                            xdt = [sbt(ph, f"xdt{i}", [128, 2, 512], BF16) for i in range(4)]
                            toff = [sbt(ph, f"toff{i}", [128, 2, 512], BF16) for i in range(4)]
                            xsD = [sbt(ph, f"xsD{i}", [128, 512], BF16) for i in range(4)]
                            yg = [sbt(ph, f"yg{i}", [128, 512], F32) for i in range(4)]
                            yn = [sbt(ph, f"yn{i}", [128, 512], BF16) for i in range(4)]
                            stt = [sbt(ph, f"stt{i}", [128, 4], F32) for i in range(4)]
                            B_in = P.buf("ssdin")
                            B_prevb = P.bufs(16, "prevb")
                            B_prevf = P.bufs(16, "prevf")
                            P.dma("sp", bTt[:], bT_d[g], reads=[B_bT_d], writes=[B_in])
                            P.dma("sp", cTt[:], cT_d[g], reads=[B_cT_d], writes=[B_in])
                            P.dma("sp", btk[:], btok_d.rearrange("(i p) c -> p i c", p=128)[:, :, 128 * g:128 * g + 128],
                                  reads=[B_btok_d], writes=[B_in])
                            xcnt = [0]

                            def load_xs(c):
                                k = xcnt[0] % 8
                                xcnt[0] += 1
                                P.dma("sp", xsb[k][:], xs_d[128 * c:128 * c + 128, 512 * g:512 * g + 512], reads=[B_xs_d], writes=[B_xs[k]])
                                return xsb[k], B_xs[k]


                            def hsel(t3, c, d):
                                return t3[:, c, 16 * d + 8 * g:16 * d + 8 * g + 8]

                            with contextlib.ExitStack() as pp:
                                psSt = [pst(pp, f"psSt{i}", [128, 512], F32) for i in range(2)]
                                B_psSt = P.bufs(2, "psSt")
                                B_H = P.bufs(2, "H")
                                B_xdd = P.bufs(2, "xdd")
                                for d in range(2):
                                    P.add("dve", R.memset(Hst[d][:], 0.0), [], [B_H[d]])

                                def chain(d):
                                    H = Hst[d]
                                    prev, Bprev = (prevf, B_prevf) if d == 0 else (prevb, B_prevb)
                                    for step in range(16):
                                        c = step if d == 0 else 15 - step
                                        P.add("act", R.activation(out=prev[:, c, :], in_=H[:], func=AF.Copy), [B_H[d]], [Bprev[c]])
                                        yield
                                        if step == 15:
                                            break
                                        xsc, Bxsc = load_xs(c)
                                        P.add("pool", R.tensor_tensor(out=xdd[d][:].rearrange("p (h q) -> p h q", h=8),
                                                                      in0=xsc[:].rearrange("p (h q) -> p h q", h=8),
                                                                      in1=hsel(Wc, c, d).unsqueeze(2).to_broadcast([128, 8, 64]), op=ALU.mult),
                                              [Bxsc, B_E], [B_xdd[d]])
                                        yield
                                        P.add("pe", R.matmul(out=psSt[d][:], lhsT=btk[:, c, :], rhs=xdd[d][:], start=True, stop=True),
                                              [B_in, B_xdd[d]], [B_psSt[d]])
                                        yield
                                        P.add("dve", R.tensor_tensor(out=H[:].rearrange("p (h q) -> p h q", h=8),
                                                                     in0=H[:].rearrange("p (h q) -> p h q", h=8),
                                                                     in1=hsel(Etot, c, d).unsqueeze(2).to_broadcast([128, 8, 64]), op=ALU.mult),
                                              [B_H[d], B_E], [B_H[d]])
                                        yield
                                        P.add("dve", R.tensor_tensor(out=H[:], in0=H[:], in1=psSt[d][:], op=ALU.add), [B_H[d], B_psSt[d]], [B_H[d]])
                                        yield

                                run_rr([chain(0), chain(1)], offset=2)
                                P.barrier()

                            with contextlib.ExitStack() as pp:
                                psSeg2 = [pst(pp, f"psSeg{i}", [128, 4, 128], F32) for i in range(2)]
                                psSeg = psSeg2 * 2
                                psOff1 = pst(pp, "psOff", [128, 512], F32)
                                psOff = [psOff1] * 4
                                psY = [pst(pp, f"psY{i}", [128, 512], F32) for i in range(4)]
                                psTr_ = pst(pp, "psTr", [128, 4, 128], BF16)
                                B_psSeg = P.bufs(2, "psSeg") * 2
                                B_psOff = P.bufs(1, "psOff") * 4
                                B_psY = P.bufs(4, "psY")
                                B_psCB = B_psY
                                B_psTr = P.bufs(1, "psTr") * 4
                                B_R = P.bufs(8, "R")
                                B_dec = P.bufs(8, "dec")
                                B_mix = P.bufs(8, "mix")
                                B_cbm = P.bufs(4, "cbm")
                                B_xdt = P.bufs(4, "xdt")
                                B_toff = P.bufs(4, "toff")
                                B_xsD = P.bufs(4, "xsD")
                                B_yg = P.bufs(4, "yg")
                                B_yn = P.bufs(4, "yn")
                                B_stt = P.bufs(4, "stt")
                                B_ssmT = P.bufs(4, "ssmT")

                                def psCB(i):
                                    return psY[i][:, 0:128]

                                def psTr(i):
                                    return psTr_[:]

                                def ychunk(c):
                                    i = c % 4
                                    cs = slice(128 * c, 128 * c + 128)
                                    xsc, Bxsc = load_xs(c)
                                    P.dma("sp", ztb[i][:], z_d[128 * c:128 * c + 128, 512 * g:512 * g + 512], reads=[B_z_d], writes=[B_zt[i]])
                                    P.add("pe", R.matmul(out=psCB(i), lhsT=bTt[:, cs], rhs=cTt[:, cs], start=True, stop=True), [B_in], [B_psCB[i]])
                                    yield
                                    P.add("dve", R.tensor_tensor(out=cbm[i][:], in0=psCB(i).unsqueeze(1).to_broadcast([128, 2, 128]),
                                                                 in1=cstb[:, 1:3, :], op=ALU.mult), [B_psCB[i], B_cst], [B_cbm[i]])
                                    yield
                                    P.add("dve", R.tensor_tensor(
                                        out=xdt[i][:].rearrange("p d (h q) -> p d h q", h=8),
                                        in0=xsc[:].rearrange("p (h q) -> p h q", h=8).unsqueeze(1).to_broadcast([128, 2, 8, 64]),
                                        in1=dt_all[:, c, :].rearrange("p (d h) -> p d h", d=2)[:, :, 8 * g:8 * g + 8].unsqueeze(3).to_broadcast([128, 2, 8, 64]),
                                        op=ALU.mult), [Bxsc, B_dt], [B_xdt[i]])
                                    yield
                                    P.add("dve", R.tensor_tensor(out=xsD[i][:], in0=xsc[:], in1=dsk[:, 512 * g:512 * g + 512], op=ALU.mult),
                                          [Bxsc, B_prm], [B_xsD[i]])
                                    yield
                                    P.add("pe", R.matmul(out=psY[i][:], lhsT=identb, rhs=xsD[i][:], start=True, stop=False),
                                          [B_cst, B_xsD[i]], [B_psY[i]])
                                    yield
                                    n4 = 0
                                    for d in range(2):
                                        for hh in range(2):
                                            r = 2 * i + (n4 % 2)
                                            n4 += 1
                                            hs4 = slice(4 * hh, 4 * hh + 4)
                                            P.add("pool", R.tensor_tensor(out=Rm[r][:, 0:4, :],
                                                                          in0=hsel(adt, c, d)[:, hs4].unsqueeze(2).to_broadcast([128, 4, 128]),
                                                                          in1=cstb[:, 1 + d, :].unsqueeze(1).to_broadcast([128, 4, 128]), op=ALU.mult),
                                                  [B_dt, B_cst], [B_R[r]])
                                            yield
                                            P.add("pe", R.matmul(out=psSeg[i][:], lhsT=cstb[:, 3 + d, :], rhs=Rm[r][:, 0:4, :], start=True, stop=True),
                                                  [B_cst, B_R[r]], [B_psSeg[i]])
                                            yield
                                            P.add("act", R.activation(out=dec[r][:, 0:4, :], in_=psSeg[i][:], func=AF.Exp), [B_psSeg[i]], [B_dec[r]])
                                            yield
                                            P.add("dve", R.tensor_tensor(out=mixm[r][:, 0:4, :], in0=dec[r][:, 0:4, :],
                                                                         in1=cbm[i][:, d, :].unsqueeze(1).to_broadcast([128, 4, 128]), op=ALU.mult),
                                                  [B_dec[r], B_cbm[i]], [B_mix[r]])
                                            yield
                                            for h4 in range(4):
                                                h = 4 * hh + h4
                                                P.add("pe", R.matmul(out=psY[i][:, 64 * h:64 * h + 64], lhsT=mixm[r][:, h4, :],
                                                                     rhs=xdt[i][:, d, 64 * h:64 * h + 64], start=False, stop=False),
                                                      [B_mix[r], B_xdt[i]], [B_psY[i]])
                                            yield
                                    for d in range(2):
                                        prev, Bprev = (prevf, B_prevf) if d == 0 else (prevb, B_prevb)
                                        P.add("pe", R.matmul(out=psOff[i][:], lhsT=cTt[:, cs], rhs=prev[:, c, :], start=True, stop=True),
                                              [B_in, Bprev[c]], [B_psOff[i]])
                                        yield
                                        P.add("dve", R.tensor_tensor(
                                            out=toff[i][:, d, :].rearrange("p (h q) -> p h q", h=8),
                                            in0=psOff[i][:].rearrange("p (h q) -> p h q", h=8),
                                            in1=hsel(Ec, c, d).unsqueeze(2).to_broadcast([128, 8, 64]),
                                            op=ALU.mult), [B_psOff[i], B_E], [B_toff[i]])
                                        yield
                                        P.add("pe", R.matmul(out=psY[i][:], lhsT=identb, rhs=toff[i][:, d, :], start=False, stop=(d == 1)),
                                              [B_cst, B_toff[i]], [B_psY[i]])
                                        yield
                                    P.add("dve", R.tensor_tensor(out=yg[i][:], in0=psY[i][:], in1=ztb[i][:], op=ALU.mult),
                                          [B_psY[i], B_zt[i]], [B_yg[i]])
                                    P.add("dve", R.memset(stt[i][:], 0.0), [], [B_stt[i]])
                                    yield
                                    P.add("act", R.activation(out=yn[i][:], in_=yg[i][:], func=AF.Square, accum_out=stt[i][:, 0:1]),
                                          [B_yg[i], B_stt[i]], [B_yn[i], B_stt[i]])
                                    yield
                                    P.add("act", R.activation(out=stt[i][:, 1:2], in_=stt[i][:, 0:1], func=AF.Ln, scale=1.0 / 512, bias=epsc[:]),
                                          [B_stt[i], B_cst], [B_stt[i]])
                                    yield
                                    P.add("act", R.activation(out=stt[i][:, 2:3], in_=stt[i][:, 1:2], func=AF.Exp, scale=-0.5), [B_stt[i]], [B_stt[i]])
                                    yield
                                    P.add("act", R.activation(out=yn[i][:], in_=yg[i][:], func=AF.Copy, scale=stt[i][:, 2:3]),
                                          [B_yg[i], B_stt[i]], [B_yn[i]])
                                    yield
                                    for q4 in range(4):
                                        P.add("pe", R.transpose(out=psTr(i)[:, q4, :], in_=yn[i][:, 128 * q4:128 * q4 + 128], identity=identb),
                                              [B_yn[i], B_cst], [B_psTr[i]])
                                    yield
                                    P.add("dve", R.tensor_copy(out=ssmT[i][:], in_=psTr(i)), [B_psTr[i]], [B_ssmT[i]])
                                    P.dma("sp", mixT_d[512 * g:512 * g + 512, cs].rearrange("(c p) t -> p c t", p=128), ssmT[i][:],
                                          reads=[B_ssmT[i]], writes=[B_mixT_d])
                                    yield

                                def ythread(t):
                                    for c in range(t, 16, 4):
                                        yield from ychunk(c)

                                run_rr([ythread(0), ythread(1), ythread(2), ythread(3)], offset=9)
                                P.barrier()

                    ssd_sc.close()
                    if debug == "ssd":
                        break

                    with contextlib.ExitStack() as ph:
                        wo = sbt(ph, "wo", [128, 8, 16, 128], BF16)
                        mt = [sbt(ph, f"mt{i}", [128, 16, 512], BF16) for i in range(2)]
                        po = [pst(ph, f"po{i}", [128, 512], F32) for i in range(4)]
                        B_wo = P.bufs(8, "wo")
                        B_mt = P.bufs(2, "mt")
                        B_po = P.bufs(4, "po")
                        for oc in range(8):
                            P.dma("sp", wo[:, oc], wout_s[l, :, oc], reads=[B_wscr], writes=[B_wo[oc]])
                        n = 0
                        P.dma("sp", mt[0][:], mixT_d.rearrange("(k p) t -> p k t", p=128)[:, :, 0:512], reads=[B_mixT_d], writes=[B_mt[0]])
                        for t in range(4):
                            i = t % 2
                            ts = slice(512 * t, 512 * t + 512)
                            if t + 1 < 4:
                                P.dma("sp", mt[1 - i][:], mixT_d.rearrange("(k p) t -> p k t", p=128)[:, :, 512 * (t + 1):512 * (t + 2)],
                                      reads=[B_mixT_d], writes=[B_mt[1 - i]])
                            for oc in range(8):
                                b = n % 4
                                n += 1
                                for k in range(16):
                                    P.add("pe", R.matmul(out=po[b][:], lhsT=wo[:, oc, k, :], rhs=mt[i][:, k, :],
                                                                                          start=(k == 0), stop=(k == 15)), [B_wo[oc], B_mt[i]], [B_po[b]])
                                P.add("dve", R.tensor_tensor(out=xT[:, oc, ts], in0=xT[:, oc, ts], in1=po[b][:], op=ALU.add),
                                      [B_po[b], B_xT[t]], [B_xT[t]])
                        P.barrier()

                    if debug == "outproj":
                        break

                    wdsc = contextlib.ExitStack()
                    wd = sbt(wdsc, "wd", [128, 8, 22, 128], BF16)
                    B_wd = P.buf("wd")
                    P.dma("sp", wd[:], wdn_s[l], reads=[B_wscr], writes=[B_wd])
                    hsc = contextlib.ExitStack()
                    hT = sbt(hsc, "hT", [128, 8, L], BF16)
                    B_hT = P.bufs(4, "hT")
                    with contextlib.ExitStack() as ph:
                        def to_h2(t, ts, rs, Brs):
                            P.add("dve", R.tensor_tensor(out=hT[:, :, ts], in0=xT[:, :, ts],
                                                                   in1=rs[:].unsqueeze(1).to_broadcast([128, 8, 512]),
                                                                   op=ALU.mult), [B_xT[t], Brs], [B_hT[t]])
                        rmsnorm_to(ph, to_h2, B_hT, "b")
                        P.barrier()

                    with contextlib.ExitStack() as ph:
                        wg = [sbt(ph, f"wg{i}", [128, 8, 128], BF16) for i in range(2)]
                        wu = [sbt(ph, f"wu{i}", [128, 8, 128], BF16) for i in range(2)]
                        pre = [sbt(ph, f"fpre{i}", [128, L + 2], BF16) for i in range(2)]
                        sg = [sbt(ph, f"sg{i}", [128, L], F32) for i in range(1)]
                        aTt = [sbt(ph, f"aTt{i}", [128, L], BF16) for i in range(2)]
                        dg3 = [sbt(ph, f"dg3{i}", [128, 3, 128], BF16) for i in range(2)]
                        pG = [pst(ph, f"pG{i}", [128, 2, 512], F32) for i in range(2)]
                        pU = [pst(ph, f"pU{i}", [128, 2, 512], F32) for i in range(2)]
                        B_w = P.bufs(2, "wgu")
                        B_pre = [P.bufs(2, f"fpre{i}") for i in range(2)]
                        B_sg = P.bufs(2, "sg")
                        B_aTt = P.bufs(2, "aTt")
                        B_dg3 = P.bufs(2, "dg3")
                        B_pG = P.bufs(2, "pG")
                        B_pU = P.bufs(2, "pU")
                        for i in range(2):
                            P.add("pool", R.memset(pre[i][:], 0.0), [], B_pre[i])
                        P.dma("sp", wg[0][:], wup_s[l, 0], reads=[B_wscr], writes=[B_w[0]])
                        P.dma("sp", wu[0][:], wup_s[l, 22], reads=[B_wscr], writes=[B_w[0]])
                        for j in range(22):
                            i = j % 2
                            if j + 1 < 22:
                                P.dma("sp", wg[1 - i][:], wup_s[l, j + 1], reads=[B_wscr], writes=[B_w[1 - i]])
                                P.dma("sp", wu[1 - i][:], wup_s[l, 22 + j + 1], reads=[B_wscr], writes=[B_w[1 - i]])
                            P.add("dve", R.tensor_tensor(
                                out=dg3[i][:], in0=identb.unsqueeze(1).to_broadcast([128, 3, 128]),
                                in1=fcwt[:, j, :].unsqueeze(2).to_broadcast([128, 3, 128]), op=ALU.mult), [B_cst, B_prm], [B_dg3[i]])
                            for h in range(2):
                                for tt in range(2):
                                    t = 2 * h + tt
                                    for k in range(8):
                                        P.add("pe", R.matmul(
                                            out=pG[h][:, tt, :], lhsT=wg[i][:, k, :], rhs=hT[:, k, 512 * t:512 * t + 512],
                                            start=(k == 0), stop=(k == 7)), [B_w[i], B_hT[t]], [B_pG[h]])
                                P.add("act", R.activation(out=pre[i][:, 1 + 1024 * h:1 + 1024 * h + 1024],
                                                                             in_=pG[h][:].rearrange("p a b -> p (a b)"), func=AF.Copy),
                                      [B_pG[h]], [B_pre[i][h]])
                            for h in range(2):
                                for tt in range(2):
                                    t = 2 * h + tt
                                    for k in range(8):
                                        P.add("pe", R.matmul(
                                            out=pU[h][:, tt, :], lhsT=wu[i][:, k, :], rhs=hT[:, k, 512 * t:512 * t + 512],
                                            start=(k == 0), stop=(k == 7)), [B_w[i], B_hT[t]], [B_pU[h]])
                            for h in range(2):
                                for tt in range(2):
                                    t = 2 * h + tt
                                    for tap in range(3):
                                        P.add("pe", R.matmul(
                                            out=pG[h][:, tt, :], lhsT=dg3[i][:, tap, :], rhs=pre[i][:, 512 * t + tap:512 * t + tap + 512],
                                            start=(tap == 0), stop=(tap == 2)), [B_dg3[i]] + B_pre[i], [B_pG[h]])
                                hs = slice(1024 * h, 1024 * h + 1024)
                                P.add("act", R.activation(out=sg[0][:, hs], in_=pG[h][:].rearrange("p a b -> p (a b)"),
                                                                                   func=AF.Silu, bias=fcbt[:, j:j + 1]), [B_pG[h], B_prm], [B_sg[h]])
                                P.add("dve", R.tensor_tensor(out=aTt[i][:, hs], in0=sg[0][:, hs],
                                                                                         in1=pU[h][:].rearrange("p a b -> p (a b)"), op=ALU.mult),
                                      [B_sg[h], B_pU[h]], [B_aTt[i]])
                            P.dma("sp", aT_d[128 * j:128 * j + 128, :], aTt[i][:], reads=[B_aTt[i]], writes=[B_aT_d])
                        P.barrier()

                    hsc.close()
                    with contextlib.ExitStack() as ph:
                        at = [sbt(ph, f"at{i}", [128, 22, 512], BF16) for i in range(2)]
                        po = [pst(ph, f"pd{i}", [128, 512], F32) for i in range(4)]
                        B_at = P.bufs(2, "at")
                        B_po = P.bufs(4, "pd")
                        n = 0
                        P.dma("sp", at[0][:], aT_d.rearrange("(k p) t -> p k t", p=128)[:, :, 0:512], reads=[B_aT_d], writes=[B_at[0]])
                        for t in range(4):
                            i = t % 2
                            ts = slice(512 * t, 512 * t + 512)
                            if t + 1 < 4:
                                P.dma("sp", at[1 - i][:], aT_d.rearrange("(k p) t -> p k t", p=128)[:, :, 512 * (t + 1):512 * (t + 2)],
                                      reads=[B_aT_d], writes=[B_at[1 - i]])
                            for oc in range(8):
                                b = n % 4
                                n += 1
                                for k in range(22):
                                    P.add("pe", R.matmul(out=po[b][:], lhsT=wd[:, oc, k, :], rhs=at[i][:, k, :],
                                                                                          start=(k == 0), stop=(k == 21)), [B_wd, B_at[i]], [B_po[b]])
                                P.add("dve", R.tensor_tensor(out=xT[:, oc, ts], in0=xT[:, oc, ts], in1=po[b][:], op=ALU.add),
                                      [B_po[b], B_xT[t]], [B_xT[t]])
                        P.barrier()
                    wdsc.close()

            with contextlib.ExitStack() as ph:
                xn = [sbt(ph, f"xn{i}", [128, 8, 512], F32) for i in range(2)]
                ost = [sbt(ph, f"ost{i}", [128, D], F32) for i in range(2)]
                pf = [pst(ph, f"pf{i}", [128, 8, 128], F32) for i in range(2)]
                B_xn = P.bufs(2, "xn")
                B_ost = P.bufs(2, "ost")
                B_pf = P.bufs(2, "pf")
                cnt = [0]

                def fin(t, ts, rs, Brs):
                    i = t % 2
                    for k in range(8):
                        P.add("dve", R.scalar_tensor_tensor(out=xn[i][:, k, :], in0=xT[:, k, ts], scalar=fwt[:, k:k + 1],
                                                                          in1=rs[:], op0=ALU.mult, op1=ALU.mult),
                              [B_xT[t], Brs, B_cst], [B_xn[i]])
                    for q in range(4):
                        j = cnt[0] % 2
                        cnt[0] += 1
                        for k in range(8):
                            P.add("pe", R.transpose(out=pf[j][:, k, :], in_=xn[i][:, k, 128 * q:128 * q + 128],
                                                                            identity=identf), [B_xn[i], B_cst], [B_pf[j]])
                        P.add("act", R.activation(out=ost[j][:], in_=pf[j][:].rearrange("p a b -> p (a b)"), func=AF.Copy),
                              [B_pf[j]], [B_ost[j]])
                        r0 = 512 * t + 128 * q
                        P.dma("sp", out_d[s, r0:r0 + 128, :], ost[j][:], reads=[B_ost[j]])
                if debug in ("loadstore",):
                    def fin_copy():
                        for t in range(4):
                            ts = slice(512 * t, 512 * t + 512)
                            i = t % 2
                            P.add("dve", R.tensor_copy(out=xn[i][:], in_=xT[:, :, ts]), [B_xT[t]], [B_xn[i]])
                            for q in range(4):
                                j = cnt[0] % 2
                                cnt[0] += 1
                                for k in range(8):
                                    P.add("pe", R.transpose(out=pf[j][:, k, :], in_=xn[i][:, k, 128 * q:128 * q + 128],
                                                                                         identity=identf), [B_xn[i], B_cst], [B_pf[j]])
                                P.add("act", R.activation(out=ost[j][:], in_=pf[j][:].rearrange("p a b -> p (a b)"), func=AF.Copy),
                                      [B_pf[j]], [B_ost[j]])
                                r0 = 512 * t + 128 * q
                                P.dma("sp", out_d[s, r0:r0 + 128, :], ost[j][:], reads=[B_ost[j]])
                    fin_copy()
                else:
                    rmsnorm_to(ph, fin, None, "f")
                P.barrier()
        if debug is not None:
            for nm, ap_, b_ in (("xs_d", xs_d, B_xs_d), ("btok_d", btok_d, B_btok_d), ("bT_d", bT_d, B_bT_d), ("cT_d", cT_d, B_cT_d),
                                ("z_d", z_d, B_z_d), ("v_d", v_d, B_v_d), ("qT_d", qT_d, B_qT_d), ("kT_d", kT_d, B_kT_d),
                                ("mixT_d", mixT_d, B_mixT_d), ("aT_d", aT_d, B_aT_d)):
                o_ = nc.dram_tensor("dump_" + nm, list(ap_.shape), BF16, kind="ExternalOutput").ap()
                P.dma("sp", o_, ap_, reads=[b_])
        P.emit()
    return nc


def _t5_bucket_np(rel):
    import math
    half = 16
    max_exact = 8
    ret = np.where(rel > 0, half, 0)
    n = np.abs(rel)
    nf = np.maximum(n, 1).astype(np.float32)
    large = max_exact + (np.log(nf / max_exact) / math.log(128 / max_exact) * (half - max_exact)).astype(np.int32)
    large = np.minimum(large, half - 1)
    return ret + np.where(n < max_exact, n, large)


def prep_inputs(inp, nlayers=DEPTH):
    f = np.float32
    NL = nlayers
    kk = np.arange(128)
    cst = np.zeros((128, 6, 128), f)
    cst[:, 0] = np.eye(128, dtype=f)
    cst[:, 1] = (kk[:, None] <= kk[None, :])
    cst[:, 2] = (kk[:, None] >= kk[None, :])
    cst[:, 3] = (kk[:, None] > kk[None, :])
    cst[:, 4] = (kk[:, None] < kk[None, :])
    cst[:, 5] = 1.0
    rel = np.arange(384)[None, :] - 128 - np.arange(128)[:, None]
    bucket = _t5_bucket_np(rel)
    bb = np.ascontiguousarray(np.asarray(inp["rel_bias"], f)[bucket].transpose(0, 2, 1))
    amask = np.where(np.abs(rel) <= 128, 0.0, -30000.0).astype(f)

    w_in = np.asarray(inp["w_in"], f)[:NL]
    Z0, X0, DT0, Q0, K0, V0 = 0, 1024, 2560, 2592, 3616, 3872
    fm_cols = np.concatenate([np.arange(X0, X0 + 1536), np.arange(Q0, Q0 + 1024), np.arange(K0, K0 + 256)])
    tm_cols = np.concatenate([np.arange(Z0, Z0 + 1024), np.arange(V0, V0 + 256), np.arange(DT0, DT0 + 32)])
    wfm = w_in[:, :, fm_cols].reshape(NL, 8, 128, 22, 128).transpose(0, 3, 2, 1, 4)
    wtm = w_in[:, :, tm_cols].reshape(NL, 8, 128, 1312).transpose(0, 2, 1, 3)
    w_out = np.asarray(inp["w_out"], f)[:NL]
    wout = w_out.reshape(NL, 16, 128, 8, 128).transpose(0, 2, 3, 1, 4)
    w_up = np.asarray(inp["w_up"], f)[:NL]
    wup = w_up.reshape(NL, 8, 128, 44, 128).transpose(0, 3, 2, 1, 4)
    w_dn = np.asarray(inp["w_down"], f)[:NL]
    wdn = w_dn.reshape(NL, 22, 128, 8, 128).transpose(0, 2, 3, 1, 4)

    def pcol(v, nch):
        return np.ascontiguousarray(np.asarray(v, f)[:NL].reshape(NL, nch, 128).transpose(0, 2, 1))

    def bc(v):
        v = np.asarray(v, f)[:NL]
        return np.ascontiguousarray(np.broadcast_to(v[:, None, :], (NL, 128, v.shape[-1])))

    sw = np.concatenate([pcol(inp["ssm_norm_w"], 8), np.ones((NL, 128, 8), f)], axis=2)
    cw = np.ascontiguousarray(np.asarray(inp["conv_w"], f)[:NL].reshape(NL, 7, 12, 128).transpose(0, 3, 2, 1))
    fcw = np.ascontiguousarray(np.asarray(inp["ffn_conv_w"], f)[:NL].reshape(NL, 3, 22, 128).transpose(0, 3, 2, 1))
    shared = {
        "cst": cst, "bb": bb, "amask": amask,
        "wfm": np.ascontiguousarray(wfm), "wtm": np.ascontiguousarray(wtm), "wout": np.ascontiguousarray(wout),
        "wup": np.ascontiguousarray(wup), "wdn": np.ascontiguousarray(wdn),
        "nw1": pcol(inp["norm1_w"], 8), "nw2": pcol(inp["norm2_w"], 8), "sw": np.ascontiguousarray(sw),
        "cw": cw, "cb": pcol(inp["conv_b"], 12),
        "dtb": bc(np.asarray(inp["dt_bias"], f).reshape(-1, 32)), "alog": bc(np.asarray(inp["a_log"], f).reshape(-1, 32)),
        "dsk": bc(np.repeat(np.asarray(inp["d_skip"], f), 64, axis=1)),
        "sink": bc(inp["attn_sink"]),
        "fcw": fcw, "fcb": pcol(inp["ffn_conv_b"], 22),
        "fw": np.ascontiguousarray(np.asarray(inp["final_norm_w"], f).reshape(8, 128).T),
    }
    return shared


_CACHE = {}


def kernel(**inputs):
    x = np.asarray(inputs["x"], np.float32)
    nseq = x.shape[0] // NCORES
    shared = prep_inputs(inputs)
    key = ("full", nseq)
    if key not in _CACHE:
        _CACHE[key] = build_program(nseq=nseq, nlayers=DEPTH)
    nc = _CACHE[key]
    in_maps = []
    for c in range(NCORES):
        m = dict(shared)
        m["x"] = np.ascontiguousarray(x[c * nseq:(c + 1) * nseq])
        in_maps.append(m)
    res = run_bass_kernel_spmd(nc, in_maps, core_ids=list(range(NCORES)))
    return np.concatenate([np.asarray(r["out"], np.float32) for r in res.results], axis=0)
```
